# Optimizing a Trainium2 kernel written in Bass

```python
import jax, jax.numpy as jnp
from jax import lax
import numpy as np

D_MODEL = 1024
BATCH = 4
SEQ = 8192
DEPTH = 2

N_MIXERS = 2
HEAD_DIM = 64
N_Q_HEADS = D_MODEL // HEAD_DIM
N_KV_HEADS = N_Q_HEADS // 4
GROUP = N_Q_HEADS // N_KV_HEADS
Q_DIM = N_Q_HEADS * HEAD_DIM
KV_DIM = N_KV_HEADS * HEAD_DIM
QKV_DIM = Q_DIM + 2 * KV_DIM
WINDOW = 128
BLOCK = 128
ROT_DIM = HEAD_DIM // 4
ROPE_THETA = 500000.0
NEG_INF = -1e30
HGRN_DK = 128
HGRN_HEADS = D_MODEL // HGRN_DK
HGRN_DV = D_MODEL // HGRN_HEADS
CHUNK = 64
D_FF = 4 * D_MODEL
NORM_EPS = 1e-5
N_ATTN_LAYERS = (DEPTH + N_MIXERS - 1) // N_MIXERS
N_HGRN_LAYERS = DEPTH // N_MIXERS

kernel_name = "hybrid_swa_sink_hgrn2_sqrelu"


def rmsnorm(x, gain):
    xf = x.astype(jnp.float32)
    y = xf * lax.rsqrt(jnp.mean(jnp.square(xf), axis=-1, keepdims=True) + NORM_EPS)
    return (y * gain.astype(jnp.float32)).astype(x.dtype)


def rotary_tables(positions):
    inv_freq = ROPE_THETA ** (-jnp.arange(0, ROT_DIM, 2, dtype=jnp.float32) / ROT_DIM)
    ang = positions.astype(jnp.float32)[..., None] * inv_freq
    return jnp.cos(ang), jnp.sin(ang)


def apply_partial_rotary(x, cos, sin):
    half = ROT_DIM // 2
    cos = cos.astype(x.dtype)
    sin = sin.astype(x.dtype)
    x1 = x[..., :half]
    x2 = x[..., half:ROT_DIM]
    return jnp.concatenate([x1 * cos - x2 * sin, x2 * cos + x1 * sin, x[..., ROT_DIM:]], axis=-1)


def sliding_window_attention(h, positions, w_qkv, b_qkv, sinks, w_o):
    bsz, seq, _ = h.shape
    nb = seq // BLOCK
    qkv = h @ w_qkv + b_qkv
    q = qkv[..., :Q_DIM].reshape(bsz, seq, N_KV_HEADS, GROUP, HEAD_DIM)
    k = qkv[..., Q_DIM:Q_DIM + KV_DIM].reshape(bsz, seq, N_KV_HEADS, HEAD_DIM)
    v = qkv[..., Q_DIM + KV_DIM:].reshape(bsz, seq, N_KV_HEADS, HEAD_DIM)
    cos, sin = rotary_tables(positions)
    q = apply_partial_rotary(q, cos[:, :, None, None, :], sin[:, :, None, None, :])
    k = apply_partial_rotary(k, cos[:, :, None, :], sin[:, :, None, :])
    qb = q.reshape(bsz, nb, BLOCK, N_KV_HEADS, GROUP, HEAD_DIM)

    def band(t):
        tp = jnp.pad(t, ((0, 0), (BLOCK, 0), (0, 0), (0, 0)))
        tp = tp.reshape(bsz, nb + 1, BLOCK, N_KV_HEADS, HEAD_DIM)
        return jnp.concatenate([tp[:, :-1], tp[:, 1:]], axis=2)

    kb = band(k)
    vb = band(v)
    scores = jnp.einsum('bnqkgd,bnskd->bnkgqs', qb, kb).astype(jnp.float32) * (HEAD_DIM ** -0.5)
    qi = jnp.arange(BLOCK)[:, None]
    kj = jnp.arange(2 * BLOCK)[None, :]
    delta = qi + BLOCK - kj
    key_pos = jnp.arange(nb)[:, None, None] * BLOCK + kj[None] - BLOCK
    mask = (delta >= 0) & (delta < WINDOW) & (key_pos >= 0)
    scores = jnp.where(mask[None, :, None, None], scores, NEG_INF)
    sink = sinks.astype(jnp.float32).reshape(1, 1, N_KV_HEADS, GROUP, 1, 1)
    m = jnp.maximum(jnp.max(scores, axis=-1, keepdims=True), sink)
    e = jnp.exp(scores - m)
    probs = e / (jnp.sum(e, axis=-1, keepdims=True) + jnp.exp(sink - m))
    out = jnp.einsum('bnkgqs,bnskd->bnqkgd', probs.astype(vb.dtype), vb)
    return out.reshape(bsz, seq, Q_DIM) @ w_o


def hgrn2_recurrence(h, lower_bound, w_in, g_norm, w_o):
    bsz, seq, _ = h.shape
    nc = seq // CHUNK
    q, f, i, g = jnp.split(h @ w_in, 4, axis=-1)
    q = jax.nn.silu(q.astype(jnp.float32))
    lb = lower_bound.astype(jnp.float32)
    forget = lb + (1.0 - lb) * jax.nn.sigmoid(f.astype(jnp.float32))
    k = 1.0 - forget
    log_f = jnp.log(forget)

    def to_chunks(t, d):
        return t.reshape(bsz, nc, CHUNK, HGRN_HEADS, d).transpose(1, 0, 3, 2, 4)

    xs = (to_chunks(q, HGRN_DK), to_chunks(k, HGRN_DK),
          to_chunks(i.astype(jnp.float32), HGRN_DV), to_chunks(log_f, HGRN_DK))
    causal = jnp.tril(jnp.ones((CHUNK, CHUNK), dtype=bool))

    def chunk_step(state, inp):
        qc, kc, vc, lc = inp
        b = jnp.cumsum(lc, axis=2)
        o_inter = jnp.einsum('bhtd,bhde->bhte', qc * jnp.exp(b), state)
        diff = b[:, :, :, None, :] - b[:, :, None, :, :]
        decay = jnp.exp(jnp.where(causal[None, None, :, :, None], diff, -jnp.inf))
        scores = jnp.einsum('bhtd,bhtsd,bhsd->bhts', qc, decay, kc)
        o_intra = jnp.einsum('bhts,bhse->bhte', scores, vc)
        b_last = b[:, :, -1]
        new_state = jnp.exp(b_last)[..., None] * state + jnp.einsum(
            'bhsd,bhse->bhde', kc * jnp.exp(b_last[:, :, None, :] - b), vc)
        return new_state, o_inter + o_intra

    state0 = jnp.zeros((bsz, HGRN_HEADS, HGRN_DK, HGRN_DV), jnp.float32)
    _, o = lax.scan(chunk_step, state0, xs)
    o = o.transpose(1, 0, 3, 2, 4).reshape(bsz, seq, HGRN_HEADS * HGRN_DV)
    o = rmsnorm(o, g_norm) * jax.nn.silu(g.astype(jnp.float32))
    return o.astype(h.dtype) @ w_o


def sqrelu_mlp(h, w_up, w_down):
    return jnp.square(jax.nn.relu(h @ w_up)) @ w_down


def setup_inputs(seed: int = 0) -> dict:
    key = jax.random.key(seed)
    ks = jax.random.split(key, 16)
    f32 = jnp.float32

    def nrm(k, shape, scale):
        return jax.random.normal(k, shape, f32) * scale

    x = jax.random.normal(ks[0], (BATCH, SEQ, D_MODEL), f32)
    positions = jnp.broadcast_to(jnp.arange(SEQ, dtype=jnp.int32)[None, :], (BATCH, SEQ))
    return {
        "x": x,
        "positions": positions,
        "mix_norm": 1.0 + nrm(ks[1], (DEPTH, D_MODEL), 0.02),
        "mlp_norm": 1.0 + nrm(ks[2], (DEPTH, D_MODEL), 0.02),
        "final_norm": 1.0 + nrm(ks[3], (D_MODEL,), 0.02),
        "attn_w_qkv": nrm(ks[4], (N_ATTN_LAYERS, D_MODEL, QKV_DIM), D_MODEL ** -0.5),
        "attn_b_qkv": nrm(ks[5], (N_ATTN_LAYERS, QKV_DIM), 0.02),
        "attn_sinks": nrm(ks[6], (N_ATTN_LAYERS, N_Q_HEADS), 0.5),
        "attn_w_o": nrm(ks[7], (N_ATTN_LAYERS, Q_DIM, D_MODEL), Q_DIM ** -0.5),
        "hgrn_w_in": nrm(ks[8], (N_HGRN_LAYERS, D_MODEL, 4 * D_MODEL), D_MODEL ** -0.5),
        "hgrn_g_norm": 1.0 + nrm(ks[9], (N_HGRN_LAYERS, HGRN_HEADS * HGRN_DV), 0.02),
        "hgrn_w_o": nrm(ks[10], (N_HGRN_LAYERS, HGRN_HEADS * HGRN_DV, D_MODEL), D_MODEL ** -0.5),
        "hgrn_lower_bounds": nrm(ks[11], (DEPTH, HGRN_HEADS * HGRN_DK), 0.1),
        "mlp_w_up": nrm(ks[12], (DEPTH, D_MODEL, D_FF), D_MODEL ** -0.5),
        "mlp_w_down": nrm(ks[13], (DEPTH, D_FF, D_MODEL), D_FF ** -0.5),
    }


def reference(x, positions, mix_norm, mlp_norm, final_norm, attn_w_qkv, attn_b_qkv, attn_sinks,
              attn_w_o, hgrn_w_in, hgrn_g_norm, hgrn_w_o, hgrn_lower_bounds, mlp_w_up, mlp_w_down):
    lbs = jnp.cumsum(jax.nn.softmax(hgrn_lower_bounds.astype(jnp.float32), axis=0), axis=0)
    lbs = lbs - lbs[0:1]
    for layer in range(DEPTH):
        j = layer // N_MIXERS
        h = rmsnorm(x, mix_norm[layer])
        if layer % N_MIXERS == 0:
            y = sliding_window_attention(h, positions, attn_w_qkv[j], attn_b_qkv[j],
                                         attn_sinks[j], attn_w_o[j])
        else:
            y = hgrn2_recurrence(h, lbs[layer], hgrn_w_in[j], hgrn_g_norm[j], hgrn_w_o[j])
        x = x + y.astype(x.dtype)
        h = rmsnorm(x, mlp_norm[layer])
        x = x + sqrelu_mlp(h, mlp_w_up[layer], mlp_w_down[layer]).astype(x.dtype)
    return rmsnorm(x, final_norm)
```

```python
import math
import os
from contextlib import ExitStack
import numpy as np
import concourse.bass as bass
import concourse.mybir as mybir
from concourse.bass_utils import run_bass_kernel_spmd

F32 = mybir.dt.float32
BF16 = mybir.dt.bfloat16
I32 = mybir.dt.int32
AF = mybir.ActivationFunctionType
ALU = mybir.AluOpType

NCORES = 8
D = 1024
SEQ = 8192
NTOK = 4096
T = 512
NT = NTOK // T
DFF = 4096
EPS = 1e-5
ARENA_WORDS = 17664
SAME_ENGINE_SYNC = bool(int(os.environ.get("SES", "1")))
HG_DBG = int(os.environ.get("HG_DBG", "9"))
HG_VAR = int(os.environ.get("HG_VAR", "1"))

ENGS = ("pe", "act", "dve", "pool", "sp")


class Op:
    __slots__ = ("eng", "idx", "fn", "waits", "marked", "dma_key", "dma_val", "phase", "dma_inc")


class Sched:
    def __init__(self):
        self.q = {e: [] for e in ENGS}
        self.res = {}
        self.seen = {e: {f: -1 for f in ENGS} for e in ENGS}
        self.seen_dma = {e: {} for e in ENGS}
        self.dma_count = {}
        self.phase = 0

    def _add(self, eng, fn, reads, writes, dma_key=None, dma_inc=16):
        op = Op()
        op.dma_inc = dma_inc
        op.eng, op.fn, op.waits, op.marked = eng, fn, [], False
        op.idx = len(self.q[eng])
        op.dma_key = dma_key
        op.phase = self.phase
        if dma_key is not None:
            self.dma_count[dma_key] = self.dma_count.get(dma_key, 0) + dma_inc
            op.dma_val = self.dma_count[dma_key]
        deps = {}
        for k in reads:
            r = self.res.setdefault(k, [None, []])
            if r[0] is not None:
                deps[id(r[0])] = (r[0], True)
        for k in writes:
            r = self.res.setdefault(k, [None, []])
            if r[0] is not None:
                deps[id(r[0])] = (r[0], True)
            for o in r[1]:
                if id(o) not in deps:
                    deps[id(o)] = (o, False)
        for k in reads:
            self.res[k][1].append(op)
        for k in writes:
            self.res[k][0] = op
            self.res[k][1] = []
        for d, strong in deps.values():
            if d.dma_key is not None:
                if self.seen_dma[eng].get(d.dma_key, 0) >= d.dma_val:
                    continue
                self.seen_dma[eng][d.dma_key] = d.dma_val
                op.waits.append(("dma", d.dma_key, d.dma_val))
            elif d.eng == eng:
                if eng in ("pe", "sp") or not SAME_ENGINE_SYNC:
                    continue
                if self.seen[eng][eng] >= d.idx:
                    continue
                self.seen[eng][eng] = d.idx
                d.marked = True
                op.waits.append(("eng", d))
            else:
                if self.seen[eng][d.eng] >= d.idx:
                    continue
                self.seen[eng][d.eng] = d.idx
                d.marked = True
                op.waits.append(("eng", d))
        self.q[eng].append(op)
        return op

    def op(self, eng, fn, reads=(), writes=()):
        return self._add(eng, fn, reads, writes)

    def dma(self, eng, out, in_, reads, writes, key):
        return self._add(eng, lambda e: e.dma_start(out=out, in_=in_), reads, writes, dma_key=key)

    def cc(self, eng, fn, reads, writes, key):
        return self._add(eng, fn, reads, writes, dma_key=key, dma_inc=1)

    def emit(self, nc, final_waits):
        with ExitStack() as es:
            esem = {}
            for e in ("pe", "act", "dve", "pool"):
                esem[e] = es.enter_context(nc.semaphore("sem_" + e))
            dsem = {k: es.enter_context(nc.semaphore("dsem_%s" % str(k))) for k in self.dma_count}
            val = {}
            for e in ENGS:
                c = 0
                for o in self.q[e]:
                    if o.marked:
                        c += 1
                        val[id(o)] = c
            block = es.enter_context(nc.Block())

            def run(eng_name, e):
                for o in self.q[eng_name]:
                    for w in o.waits:
                        if w[0] == "dma":
                            e.wait_ge(dsem[w[1]], w[2])
                        else:
                            e.wait_ge(esem[w[1].eng], val[id(w[1])])
                    ins = o.fn(e)
                    if o.dma_key is not None:
                        if o.dma_inc == 1:
                            ins.then_inc(dsem[o.dma_key])
                        else:
                            ins.then_inc(dsem[o.dma_key], 16)
                    elif o.marked:
                        ins.then_inc(esem[eng_name], 1)
                if eng_name == "pool":
                    for k in final_waits:
                        e.wait_ge(dsem[k], self.dma_count[k])

            @block.tensor
            def _(e):
                run("pe", e)

            @block.scalar
            def _(e):
                run("act", e)

            @block.vector
            def _(e):
                run("dve", e)

            @block.gpsimd
            def _(e):
                run("pool", e)

            @block.sync
            def _(e):
                run("sp", e)


def _slab_proj(w, c0, ncols):
    s = w[:, c0:c0 + ncols].reshape(8, 128, ncols).transpose(1, 0, 2)
    return np.ascontiguousarray(s).reshape(128, 8 * ncols)


def _slab_down(w, m):
    s = w[:, m * 128:(m + 1) * 128].reshape(32, 128, 128).transpose(1, 0, 2)
    return np.ascontiguousarray(s).reshape(128, 32 * 128)


def _qhead_of_chunk(c):
    base = 0 if c < 4 else 8
    i = c % 4
    return base + i, base + 4 + i


def _fm(v):
    return np.ascontiguousarray(v.reshape(-1, 128).T)


NV = 96
V_MIX0, V_MLP0, V_MIX1, V_MLP1, V_FIN, V_BQK, V_INVF, V_GN, V_LB0, V_LB1 = 0, 8, 16, 24, 32, 40, 50, 51, 59, 67
L0_SLABS = 21
L1_SLABS = 26


def host_prepare(inp):
    wqkv = inp["attn_w_qkv"][0]
    bqkv = inp["attn_b_qkv"][0]
    qcols = []
    for c in range(8):
        a, b = _qhead_of_chunk(c)
        qcols += list(range(a * 64, a * 64 + 64)) + list(range(b * 64, b * 64 + 64))
    qcols = np.array(qcols)
    wq_perm = wqkv[:, qcols]
    slabs0 = [_slab_proj(wq_perm, 0, 512), _slab_proj(wq_perm, 512, 512), _slab_proj(wqkv, 1024, 512)]
    wo_perm = inp["attn_w_o"][0][qcols, :]
    slabs0 += [_slab_proj(wo_perm, 0, 512), _slab_proj(wo_perm, 512, 512)]
    slabs0 += [_slab_proj(inp["mlp_w_up"][0], s * 512, 512) for s in range(8)]
    slabs0 += [_slab_down(inp["mlp_w_down"][0], m) for m in range(8)]
    win = inp["hgrn_w_in"][0]
    slabs1 = [_slab_proj(win, s * 512, 512) for s in range(8)]
    slabs1 += [_slab_proj(inp["hgrn_w_o"][0], s * 512, 512) for s in range(2)]
    slabs1 += [_slab_proj(inp["mlp_w_up"][1], s * 512, 512) for s in range(8)]
    slabs1 += [_slab_down(inp["mlp_w_down"][1], m) for m in range(8)]
    w0 = np.stack(slabs0).astype(np.float32)
    w1 = np.stack(slabs1).astype(np.float32)

    vecs = np.zeros((128, NV), np.float32)
    vecs[:, V_MIX0:V_MIX0 + 8] = _fm(inp["mix_norm"][0])
    vecs[:, V_MLP0:V_MLP0 + 8] = _fm(inp["mlp_norm"][0])
    vecs[:, V_MIX1:V_MIX1 + 8] = _fm(inp["mix_norm"][1])
    vecs[:, V_MLP1:V_MLP1 + 8] = _fm(inp["mlp_norm"][1])
    vecs[:, V_FIN:V_FIN + 8] = _fm(inp["final_norm"])
    bq_perm = bqkv[qcols]
    vecs[:, V_BQK:V_BQK + 8] = _fm(bq_perm)
    vecs[:, V_BQK + 8:V_BQK + 10] = _fm(bqkv[1024:1280])
    inv_freq = (np.float32(500000.0) ** (-np.arange(0, 16, 2, dtype=np.float32) / np.float32(16))).astype(np.float32)
    invf = np.zeros(128, np.float32)
    for p in range(128):
        if p % 64 < 16:
            invf[p] = inv_freq[p % 8]
    vecs[:, V_INVF] = invf
    vecs[:, V_GN:V_GN + 8] = _fm(inp["hgrn_g_norm"][0])
    vecs[:, V_LB0:V_LB0 + 8] = _fm(inp["hgrn_lower_bounds"][0])
    vecs[:, V_LB1:V_LB1 + 8] = _fm(inp["hgrn_lower_bounds"][1])

    rows = np.zeros((1, 272), np.float32)
    rows[0, 0:256] = bqkv[1280:1536]
    rows[0, 256:272] = inp["attn_sinks"][0]

    ident = np.eye(128, dtype=np.float32)
    R = np.zeros((128, 128), np.float32)
    for h in range(2):
        for j in range(8):
            R[h * 64 + j + 8, h * 64 + j] = -1.0
            R[h * 64 + j, h * 64 + j + 8] = 1.0
    s_idx = np.arange(128)[:, None]
    q_idx = np.arange(128)[None, :]
    maskP = (s_idx > q_idx).astype(np.float32)
    maskC = (s_idx <= q_idx).astype(np.float32)

    shared = dict(w0=w0, w1=w1, vecs=vecs, rows=rows)
    per_core = []
    x = inp["x"]
    pos = inp["positions"]
    for c in range(NCORES):
        b, half = c // 2, c % 2
        xs = np.zeros((NTOK + 128, D), np.float32)
        ps = np.zeros((1, NTOK + 128), np.int32)
        if half == 1:
            xs[:] = x[b, NTOK - 128:SEQ]
            ps[0, :] = pos[b, NTOK - 128:SEQ]
            mp0 = maskP
        else:
            xs[128:] = x[b, 0:NTOK]
            ps[0, 128:] = pos[b, 0:NTOK]
            mp0 = np.zeros_like(maskP)
        cst = np.ascontiguousarray(np.stack([ident, R, maskP, maskC, mp0], axis=1))
        per_core.append(dict(xtok=xs, pos=ps, cst=cst))
    return shared, per_core


class Builder:
    def __init__(self, nc, mode):
        self.nc = nc
        self.S = Sched()
        self.mode = mode
        self.es = ExitStack()
        self.uid = 0
        self.dbg_stage = 9
        self.arena = None
        self.aoff = {1: 0, 2: 0}
        self.deferred = []
        self.deferred_hg = []
        self.pending_exchange = None
        self.pending_chunks = []
        self.x_prefetched = False
        self.xs_pref = {}
        self.fused = False

    def sb(self, name, shape, dt):
        return self.es.enter_context(self.nc.sbuf_tensor(name, list(shape), dt))

    def aalloc(self, phase, name, shape, dt):
        if self.arena is None:
            self.arena = self.sb("arena", [128, ARENA_WORDS], F32)
        esz = 4 if dt in (F32, I32) else 2
        n = 1
        for d in shape[1:]:
            n *= d
        words = (n * esz + 3) // 4
        off = self.aoff[phase]
        self.aoff[phase] = off + words
        assert self.aoff[phase] <= ARENA_WORDS, (name, self.aoff)
        v = self.arena[:, off:off + words]
        if dt != F32:
            v = v.bitcast(dt)
        if len(shape) == 3:
            v = v.rearrange("p (a b) -> p a b", a=shape[1])
        return v

    def ps(self, name, shape, dt):
        return self.es.enter_context(self.nc.psum_tensor(name, list(shape), dt))

    def setup_common(self, wspecs):
        nc, S = self.nc, self.S
        self.wsc = {}
        self.vecs_d = nc.dram_tensor("vecs", [128, NV], F32, kind="ExternalInput").ap()
        self.cst_d = nc.dram_tensor("cst", [128, 5, 128], F32, kind="ExternalInput").ap()
        self.vecs = self.sb("vecs_sb", [128, NV], F32)
        self.cstf = self.sb("cst_f", [128, 5, 128], F32)
        self.cstb = self.sb("cst_b", [128, 5, 128], BF16)
        self.ones = self.sb("ones_b", [128, 128], BF16)
        self.ones1 = self.sb("ones1_b", [128, 128], BF16)
        self.X = self.sb("X", [128, 8, T], F32)
        self.H = self.sb("H", [128, 8, T], BF16)
        self.SQ = self.sb("SQ", [128, 8, T], BF16)
        self.RSTD = self.sb("RSTD", [128, T], F32)
        self.HID = self.sb("HID", [128, 32, T], BF16)
        self.RT = [self.sb("RT%d" % i, [128, T], F32) for i in range(2)]
        self.NSLOT = 4
        self.slab = [self.sb("slab%d" % i, [128, 4096], BF16) for i in range(self.NSLOT)]
        self.slab_i = 0
        self.pb = [self.ps("pb%d" % i, [128, 512], F32) for i in range(8)]
        self.rr = {}
        self.cast_group = {}
        order = []
        for wname, gids in wspecs:
            w_in = nc.dram_tensor(wname, [len(gids), 128, 4096], F32, kind="ExternalInput").ap()
            w_sc = nc.dram_tensor(wname + "_bf", [len(gids), 128, 4096], BF16, kind="Internal").ap()
            for i, gid in enumerate(gids):
                self.wsc[gid] = w_sc[i]
                order.append((gid, w_in[i], w_sc[i]))
        self.cast_src = {gid: (src, dst) for gid, src, dst in order}
        self.cast_batch = 0
        S.dma("act", self.vecs[:], self.vecs_d, [], ["vecs"], key="misc0")
        S.dma("act", self.cstf[:], self.cst_d, [], ["cstf"], key="misc1")
        S.op("dve", lambda e: e.tensor_copy(out=self.cstb[:], in_=self.cstf[:]), ["cstf"], ["cstb"])
        S.op("dve", lambda e: e.memset(self.ones[:], 1.0 / 1024.0), [], ["ones"])
        S.op("dve", lambda e: e.memset(self.ones1[:], 1.0), [], ["ones1"])
        self.eps_t = self.sb("eps_t", [128, 1], F32)
        S.op("dve", lambda e: e.memset(self.eps_t[:], EPS), [], ["eps_t"])

    def emit_casts(self, gids):
        S = self.S
        gids = [g for g in gids if g in self.cast_src]
        if not gids:
            return
        b = self.cast_batch
        self.cast_batch += 1
        for gid in gids:
            src, dst = self.cast_src.pop(gid)
            S.dma("pool", dst, src, reads=[], writes=[("wsc", gid)], key=("cast", b))
        last = S.res[("wsc", gids[-1])][0]
        for gid in gids:
            S.res[("wsc", gid)][0] = last

    def flush(self):
        d, self.deferred = self.deferred, []
        for fn in d:
            fn()

    def next_rr(self, name, n):
        i = self.rr.get(name, 0)
        self.rr[name] = (i + 1) % n
        return i

    def load_slab(self, s):
        S = self.S
        i = self.slab_i
        self.slab_i = (i + 1) % self.NSLOT
        key = ("slab", i)
        S.dma("sp", self.slab[i][:], self.wsc[s], reads=[("wsc", s)], writes=[key],
              key=("slabsem", i))
        return self.slab[i], key

    def rmsnorm(self, gcol, n=T, src=None, srckey="X", dst=None, dstkey="H", post=None):
        S = self.S
        X = self.X if src is None else src
        H = self.H if dst is None else dst
        SQ, RSTD = self.SQ, self.RSTD
        bank = 6
        pbk = self.pb[bank]
        for c in range(8):
            S.op("act", lambda e, c=c: e.activation(out=SQ[:, c, :n], in_=X[:, c, :n], func=AF.Square),
                 [(srckey, c)], [("SQ", c)])
        for c in range(8):
            S.op("pe", lambda e, c=c: e.matmul(pbk[:, :n], lhsT=self.ones[:], rhs=SQ[:, c, :n],
                                                 start=(c == 0), stop=(c == 7)),
                 [("SQ", c), "ones"], [("pb", bank)])
        S.op("act", lambda e: e.activation(out=RSTD[:, :n], in_=pbk[:, :n], func=AF.Ln, bias=self.eps_t[:, 0:1],
                                           scale=1.0), [("pb", bank), "eps_t"], ["RSTD"])
        S.op("act", lambda e: e.activation(out=RSTD[:, :n], in_=RSTD[:, :n], func=AF.Exp, scale=-0.5),
             ["RSTD"], ["RSTD"])
        for c in range(8):
            self._rms_chunk(c, gcol, n, X, srckey, H, dstkey, post)

    def _rms_chunk(self, c, gcol, n, X, srckey, H, dstkey, post):
        S = self.S
        RSTD = self.RSTD
        if post is None:
            S.op("dve", lambda e: e.scalar_tensor_tensor(
                out=H[:, c, :n], in0=X[:, c, :n], scalar=self.vecs[:, gcol + c:gcol + c + 1],
                in1=RSTD[:, :n], op0=ALU.mult, op1=ALU.mult),
                 [(srckey, c), "RSTD", "vecs"], [(dstkey, c)])
        else:
            r = self.next_rr("rt", 2)
            rt = self.RT[r]
            S.op("dve", lambda e: e.scalar_tensor_tensor(
                out=rt[:, :n], in0=X[:, c, :n], scalar=self.vecs[:, gcol + c:gcol + c + 1],
                in1=RSTD[:, :n], op0=ALU.mult, op1=ALU.mult),
                 [(srckey, c), "RSTD", "vecs"], [("RT", r)])
            post(c, rt, ("RT", r))

    def proj_fm(self, wt, wkey, j, ncols_slab, src, srckey, nk, n, evac):
        S = self.S
        b = self.next_rr("proj", 2)
        pbk = self.pb[b]
        for kc in range(nk):
            S.op("pe", lambda e, kc=kc: e.matmul(
                pbk[:, :n], lhsT=wt[:, kc * ncols_slab + j * 128: kc * ncols_slab + (j + 1) * 128],
                rhs=src[:, kc, :n], start=(kc == 0), stop=(kc == nk - 1)),
                 [wkey, (srckey, kc)], [("pb", b)])
        self.flush()
        evac(pbk, ("pb", b))

    def mlp(self, slab0, n=T):
        S = self.S
        X, H, HID = self.X, self.H, self.HID
        for s in range(8):
            wt, wkey = self.load_slab(slab0 + s)
            for j in range(4):
                m = s * 4 + j

                def evac(pbk, pkey, m=m):
                    r = self.next_rr("rt", 2)
                    rt = self.RT[r]
                    S.op("act", lambda e: e.activation(out=rt[:, :n], in_=pbk[:, :n], func=AF.Relu),
                         [pkey], [("RT", r)])
                    S.op("pool", lambda e: e.tensor_tensor(out=HID[:, m, :n], in0=rt[:, :n], in1=rt[:, :n],
                                                           op=ALU.mult),
                         [("RT", r)], [("HID", m)])
                self.proj_fm(wt, wkey, j, 512, H, "H", 8, n, evac)
        for m in range(8):
            wt, wkey = self.load_slab(slab0 + 8 + m)

            def evac(pbk, pkey, m=m):
                S.op("dve", lambda e: e.tensor_tensor(out=X[:, m, :n], in0=pbk[:, :n], in1=X[:, m, :n],
                                                      op=ALU.add),
                     [pkey, ("X", m)], [("X", m)])
            self.proj_fm(wt, wkey, 0, 128, HID, "HID", 32, n, evac)

    def setup_l0(self):
        nc, S = self.nc, self.S
        self.x_d = nc.dram_tensor("xtok", [NTOK + 128, D], F32, kind="ExternalInput").ap()
        self.pos_d = nc.dram_tensor("pos", [1, NTOK + 128], I32, kind="ExternalInput").ap()
        self.rows_d = nc.dram_tensor("rows", [1, 272], F32, kind="ExternalInput").ap()
        A = lambda name, shape, dt: self.aalloc(1, name, shape, dt)
        self.XS = [A("XS%d" % i, [128, D], F32) for i in range(3)]
        self.Q = A("Q", [128, 8, T], BF16)
        self.KT = A("KT", [128, 2, T + 128], BF16)
        self.VT = A("VT", [128, 5, 256], BF16)
        self.AO = A("AO", [128, 8, T], BF16)
        self.QB = [A("QB%d" % i, [128, T], BF16) for i in range(2)]
        self.T1 = [A("T1%d" % i, [128, T], F32) for i in range(2)]
        self.T2 = [A("T2%d" % i, [128, T], F32) for i in range(2)]
        self.EP = [A("EP%d" % i, [128, T], BF16) for i in range(4)]
        self.PP = [A("PP%d" % i, [128, T], BF16) for i in range(4)]
        self.DEN = [A("DEN%d" % i, [128, T], F32) for i in range(2)]
        self.POSI = A("POSI", [128, T], I32)
        self.POSF = A("POSF", [128, T], F32)
        self.ANG = A("ANG", [128, T], F32)
        self.COS = A("COS", [128, T], F32)
        self.SIN = A("SIN", [128, T], F32)
        self.rows = self.sb("rows_sb", [128, 272], F32)
        self.esink = self.sb("esink", [128, 16], F32)
        self.pi_t = self.sb("pi_t", [128, 1], F32)
        S.dma("act", self.rows[:], self.rows_d.partition_broadcast(128), [], ["rows"], key="misc2")
        S.op("act", lambda e: e.activation(out=self.esink[:], in_=self.rows[:, 256:272], func=AF.Exp),
             ["rows"], ["esink"])
        S.op("dve", lambda e: e.memset(self.pi_t[:], math.pi), [], ["pi_t"])

    def load_x_fm(self, row0, nblk):
        S = self.S
        X = self.X
        ident = self.cstf[:, 0, :]
        for b in range(nblk):
            self._load_x_blk(row0, b)

    def prefetch_x(self, row0, nblk=3):
        for b in range(nblk):
            r = self.next_rr("xs", 3)
            self.S.dma("sp", self.XS[r][:], self.x_d[row0 + b * 128: row0 + (b + 1) * 128, :], [], [("XS", r)],
                       key=("xs", r))
            self.xs_pref[(row0, b)] = r

    def _load_x_blk(self, row0, b):
        S = self.S
        X = self.X
        ident = self.cstf[:, 0, :]
        if True:
            if (row0, b) in self.xs_pref:
                r = self.xs_pref.pop((row0, b))
            else:
                r = self.next_rr("xs", 3)
                S.dma("sp", self.XS[r][:], self.x_d[row0 + b * 128: row0 + (b + 1) * 128, :], [], [("XS", r)],
                      key=("xs", r))
            xs = self.XS[r]
            for hc in range(2):
                bank = (6, 2)[hc]
                pbk = self.pb[bank]
                for c4 in range(4):
                    c = hc * 4 + c4
                    S.op("pe", lambda e, c=c, c4=c4, pbk=pbk: e.transpose(
                        out=pbk[:, c4 * 128:(c4 + 1) * 128], in_=xs[:, c * 128:(c + 1) * 128], identity=ident),
                         [("XS", r), "cstf"], [("pb", bank)])
                S.op("act", lambda e, hc=hc, pbk=pbk, b=b: e.activation(
                    out=X[:, hc * 4:(hc + 1) * 4, b * 128:(b + 1) * 128],
                    in_=pbk[:, :].rearrange("p (c t) -> p c t", c=4), func=AF.Copy),
                     [("pb", bank)], [("X", hc * 4 + i) for i in range(4)])

    def rope_tables(self, col0, n):
        S = self.S
        twopi = 2.0 * math.pi
        PI_LO = 3.1415925
        ANG, U, KI = self.ANG, self.POSF, self.POSI
        S.dma("act", self.POSI[:, :n], self.pos_d[:, col0:col0 + n].partition_broadcast(128), [], ["POSI"],
              key="posi")
        S.op("dve", lambda e: e.tensor_copy(out=self.POSF[:, :n], in_=self.POSI[:, :n]), ["POSI"], ["POSF"])
        invf = self.vecs[:, V_INVF:V_INVF + 1]
        S.op("dve", lambda e: e.tensor_scalar(out=ANG[:, :n], in0=self.POSF[:, :n], scalar1=invf, scalar2=None,
                                              op0=ALU.mult), ["POSF", "vecs"], ["ANG"])
        S.op("dve", lambda e: e.tensor_scalar(out=U[:, :n], in0=ANG[:, :n], scalar1=1.0 / twopi, scalar2=None,
                                              op0=ALU.mult), ["ANG"], ["POSF"])
        S.op("dve", lambda e: e.tensor_copy(out=KI[:, :n], in_=U[:, :n]), ["POSF"], ["POSI"])
        S.op("dve", lambda e: e.tensor_copy(out=U[:, :n], in_=KI[:, :n]), ["POSI"], ["POSF"])
        S.op("dve", lambda e: e.scalar_tensor_tensor(out=ANG[:, :n], in0=U[:, :n], scalar=-twopi, in1=ANG[:, :n],
                                                     op0=ALU.mult, op1=ALU.add), ["POSF", "ANG"], ["ANG"])

        def fold(thr, op, delta):
            S.op("dve", lambda e: e.tensor_scalar(out=U[:, :n], in0=ANG[:, :n], scalar1=thr, scalar2=None, op0=op),
                 ["ANG"], ["POSF"])
            S.op("dve", lambda e: e.scalar_tensor_tensor(out=ANG[:, :n], in0=U[:, :n], scalar=delta, in1=ANG[:, :n],
                                                         op0=ALU.mult, op1=ALU.add), ["POSF", "ANG"], ["ANG"])

        def clamp():
            S.op("dve", lambda e: e.tensor_scalar(out=ANG[:, :n], in0=ANG[:, :n], scalar1=PI_LO, scalar2=-PI_LO,
                                                  op0=ALU.min, op1=ALU.max), ["ANG"], ["ANG"])
        fold(math.pi, ALU.is_gt, -twopi)
        fold(-math.pi, ALU.is_lt, twopi)
        S.op("dve", lambda e: e.tensor_scalar(out=self.COS[:, :n], in0=ANG[:, :n], scalar1=0.5 * math.pi,
                                              scalar2=None, op0=ALU.add), ["ANG"], ["COS"])
        clamp()
        S.op("act", lambda e: e.activation(out=self.SIN[:, :n], in_=ANG[:, :n], func=AF.Sin), ["ANG"], ["SIN"])
        S.op("dve", lambda e: e.tensor_copy(out=ANG[:, :n], in_=self.COS[:, :n]), ["COS", "SIN"], ["ANG"])
        fold(math.pi, ALU.is_gt, -twopi)
        clamp()
        S.op("act", lambda e: e.activation(out=self.COS[:, :n], in_=ANG[:, :n], func=AF.Sin), ["ANG"], ["COS"])

    def qk_chunk(self, wt, wkey, j, bcol, dst, dstkey, n):
        S = self.S

        def evac(pbk, pkey):
            r = self.next_rr("qb", 2)
            qb, t1, t2 = self.QB[r], self.T1[r], self.T2[r]
            S.op("act", lambda e: e.activation(out=qb[:, :n], in_=pbk[:, :n], func=AF.Identity,
                                               bias=self.vecs[:, bcol:bcol + 1], scale=1.0),
                 [pkey, "vecs"], [("QB", r)])
            bank = (6, 3)[r]
            sw = self.pb[bank]
            S.op("dve", lambda e: e.tensor_tensor(out=t1[:, :n], in0=qb[:, :n], in1=self.COS[:, :n], op=ALU.mult),
                 [("QB", r), "COS"], [("T1", r)])

            def tail():
                S.op("pe", lambda e: e.matmul(sw[:, :n], lhsT=self.cstb[:, 1, :], rhs=qb[:, :n], start=True, stop=True),
                     [("QB", r), "cstb"], [("pb", bank)])
                S.op("dve", lambda e: e.tensor_tensor(out=t2[:, :n], in0=sw[:, :n], in1=self.SIN[:, :n], op=ALU.mult),
                     [("pb", bank), "SIN"], [("T2", r)])
                S.op("pool", lambda e: e.tensor_tensor(out=dst, in0=t1[:, :n], in1=t2[:, :n], op=ALU.add),
                     [("T1", r), ("T2", r)], [dstkey])
            self.deferred.append(tail)
        self.proj_fm(wt, wkey, j, 512, self.H, "H", 8, n, evac)

    def kv_proj(self, wt, wkey, tok0, nblk, kcol0, vblk0):
        S = self.S
        n = nblk * 128
        for j in range(2):
            self.qk_chunk(wt, wkey, j, V_BQK + 8 + j, self.KT[:, j, kcol0:kcol0 + n], ("KT", j), n)
        for b in range(nblk):
            bk = self.next_rr("proj", 2)
            pbk = self.pb[bk]
            for kc in range(8):
                S.op("pe", lambda e, kc=kc, b=b, pbk=pbk: e.matmul(
                    pbk[:, :256], lhsT=self.H[:, kc, b * 128:(b + 1) * 128],
                    rhs=wt[:, kc * 512 + 256: kc * 512 + 512], start=(kc == 0), stop=(kc == 7)),
                     [wkey, ("H", kc)], [("pb", bk)])
            self.flush()
            S.op("dve", lambda e, b=b, pbk=pbk: e.tensor_tensor(
                out=self.VT[:, vblk0 + b, :], in0=pbk[:, :256], in1=self.rows[:, 0:256], op=ALU.add),
                 [("pb", bk), "rows"], [("VT", vblk0 + b)])

    def _attn_stage1(self, b, first, g, bset):
        S = self.S
        Q, KT = self.Q, self.KT
        par = g % 2
        kc = g // 2
        pr = slice(par * 64, par * 64 + 64)
        rhs_q = Q[pr, kc * 4:kc * 4 + 4, b * 128:(b + 1) * 128]
        qkeys = [("Q", kc * 4 + i) for i in range(4)]
        es = []
        for which in range(2):
            es.append(self._attn_score(b, first, which, (2, 6)[bset] + which, pr, kc, rhs_q, qkeys))
        return es

    def _attn_score(self, b, first, which, bank, pr, kc, rhs_q, qkeys):
        S = self.S
        kb = b + which
        pbk = self.pb[bank]
        S.op("pe", lambda e: e.matmul(
            pbk[:, :].rearrange("p (i t) -> p i t", i=4), lhsT=self.KT[pr, kc, kb * 128:(kb + 1) * 128],
            rhs=rhs_q, start=True, stop=True), [("KT", kc)] + qkeys, [("pb", bank)])
        r = self.next_rr("ep", 4)
        ep, pp = self.EP[r], self.PP[r]
        S.op("act", lambda e: e.activation(out=ep[:], in_=pbk[:], func=AF.Exp, scale=0.125),
             [("pb", bank)], [("EP", r)])
        mi = (4 if first and b == 0 else 2) if which == 0 else 3
        msk = self.cstb[:, mi, :].unsqueeze(1).to_broadcast([128, 4, 128])
        S.op("pool", lambda e: e.tensor_tensor(
            out=pp[:].rearrange("p (i t) -> p i t", i=4), in0=ep[:].rearrange("p (i t) -> p i t", i=4),
            in1=msk, op=ALU.mult), [("EP", r), "cstb"], [("PP", r)])
        return (pp, ("PP", r), kb)

    def _attn_stage2(self, b, g, es, oset):
        S = self.S
        VT, AO = self.VT, self.AO
        par = g % 2
        kc = g // 2
        pr = slice(par * 64, par * 64 + 64)
        vc0 = (g - par) * 64
        bo, bd = ((4, 5), (0, 1))[oset]
        for which, (pp, pkey, kb) in enumerate(es):
            S.op("pe", lambda e, pp=pp, kb=kb, which=which: e.matmul(
                self.pb[bo][:], lhsT=VT[:, kb, vc0:vc0 + 128], rhs=pp[:], start=(which == 0), stop=(which == 1)),
                 [pkey, ("VT", kb)], [("pb", bo)])
        for which, (pp, pkey, kb) in enumerate(es):
            S.op("pe", lambda e, pp=pp, which=which: e.matmul(
                self.pb[bd][:], lhsT=self.ones1[:], rhs=pp[:], start=(which == 0), stop=(which == 1)),
                 [pkey, "ones1"], [("pb", bd)])
        r = self.next_rr("den", 2)
        den = self.DEN[r]
        esk = self.esink[pr, 4 * g:4 * g + 4].unsqueeze(2).to_broadcast([64, 4, 128])
        S.op("dve", lambda e: e.tensor_tensor(
            out=den[pr, :].rearrange("p (i t) -> p i t", i=4),
            in0=self.pb[bd][pr, :].rearrange("p (i t) -> p i t", i=4), in1=esk, op=ALU.add),
             [("pb", bd), "esink"], [("DEN", r)])
        S.op("act", lambda e: e.activation(out=den[pr, :], in_=den[pr, :], func=AF.Ln), [("DEN", r)], [("DEN", r)])
        S.op("act", lambda e: e.activation(out=den[pr, :], in_=den[pr, :], func=AF.Exp, scale=-1.0),
             [("DEN", r)], [("DEN", r)])
        S.op("dve", lambda e: e.tensor_tensor(
            out=AO[pr, kc * 4:kc * 4 + 4, b * 128:(b + 1) * 128],
            in0=self.pb[bo][pr, :].rearrange("p (i t) -> p i t", i=4),
            in1=den[pr, :].rearrange("p (i t) -> p i t", i=4), op=ALU.mult),
             [("pb", bo), ("DEN", r)], [("AO", kc * 4 + i) for i in range(4)])

    def attention_tile(self, first):
        prev = None
        i = 0
        for b in range(4):
            for g in range(4):
                es = self._attn_stage1(b, first, g, i % 2)
                if prev is not None:
                    self._attn_stage2(*prev)
                prev = (b, g, es, i % 2)
                i += 1
        self._attn_stage2(*prev)

    def layer0_tile(self, t, hook_q=None, hook_mlp=None):
        S = self.S
        row0 = 128 + t * T
        self.load_x_fm(row0, 4)
        if self.dbg_stage == 0:
            return
        self.rope_tables(row0, T)
        self.rmsnorm(V_MIX0)
        for s in range(2):
            wt, wkey = self.load_slab(s)
            for j in range(4):
                c = s * 4 + j
                self.qk_chunk(wt, wkey, j, V_BQK + c, self.Q[:, c, :], ("Q", c), T)
                if c % 2 == 1:
                    self.drain_chunks(1)
        if hook_q is not None:
            hook_q()
        wt, wkey = self.load_slab(2)
        self.kv_proj(wt, wkey, 0, 4, 128, 1)
        self.attention_tile(t == 0)
        S.op("pool", lambda e: e.tensor_copy(out=self.KT[:, :, 0:128], in_=self.KT[:, :, T:T + 128]),
             [("KT", 0), ("KT", 1)], [("KT", 0), ("KT", 1)])
        S.op("pool", lambda e: e.tensor_copy(out=self.VT[:, 0, :], in_=self.VT[:, 4, :]), [("VT", 4)], [("VT", 0)])
        for s in range(2):
            wt, wkey = self.load_slab(3 + s)
            for j in range(4):
                m = s * 4 + j

                def evac(pbk, pkey, m=m):
                    S.op("dve", lambda e: e.tensor_tensor(out=self.X[:, m, :], in0=pbk[:], in1=self.X[:, m, :],
                                                          op=ALU.add),
                         [pkey, ("X", m)], [("X", m)])
                self.proj_fm(wt, wkey, j, 512, self.AO, "AO", 8, T, evac)
                if m % 2 == 1:
                    self.drain_chunks(1)
        self.drain_chunks()
        if self.dbg_stage == 1:
            return
        self.rmsnorm(V_MLP0)
        if hook_mlp is not None:
            hook_mlp()
        self.mlp(5)

    def layer0_halo(self):
        self.load_x_fm(0, 1)
        self.rope_tables(0, 128)
        self.rmsnorm(V_MIX0, n=128)
        wt, wkey = self.load_slab(2)
        self.kv_proj(wt, wkey, 0, 1, 0, 0)

    def setup_hgrn(self, state_only, sinit_ap=None):
        nc, S = self.nc, self.S
        self.state_only = state_only
        self.SF = self.sb("SF", [128, 8, 128], F32)
        self.EBL = self.sb("EBL", [128, 64], F32)
        self.LB = self.sb("LB", [128, 8], F32)
        self.OMLB = self.sb("OMLB", [128, 8], F32)
        self.RESET = self.sb("RESET", [128, T], F32)
        self.FG = [self.sb("FG%d" % i, [128, T], F32) for i in range(3)]
        self.LF = [self.sb("LF%d" % i, [128, T], F32) for i in range(3)]
        self.BC = [self.sb("BC%d" % i, [128, T], F32) for i in range(3)]
        self.KE2 = [self.sb("KE2%d" % i, [128, T], BF16) for i in range(3)]
        self.dum_pool = self.sb("dum_pool", [128, 1], F32)
        self.dum_act = self.sb("dum_act", [128, 1], F32)
        self.dum_dve = self.sb("dum_dve", [128, 1], F32)
        self.VTK = self.HID[:, 0:16, :].rearrange("p a b -> p (a b)").rearrange("p (c n) -> p c n", c=8)
        self.KET = self.HID[:, 16:32, :].rearrange("p a b -> p (a b)").rearrange("p (h c d) -> p h c d", h=8, c=8)
        if not state_only:
            A = lambda name, shape, dt: self.aalloc(2, name, shape, dt)
            self.QS = [A("QS%d" % i, [128, T], F32) for i in range(3)]
            self.EB = [A("EB%d" % i, [128, T], F32) for i in range(3)]
            self.KE = [A("KE%d" % i, [128, T], BF16) for i in range(3)]
            self.QE = A("QE", [128, 8, T], BF16)
            self.GS = A("GS", [128, 8, T], BF16)
            self.AT = A("AT", [128, 8, T], BF16)
            self.OS = A("OS", [128, 8, T], F32)
            self.SBF = A("SBF", [128, 8, 128], BF16)
            self.YT = [A("YT%d" % i, [128, D], F32) for i in range(2)]
        S.op("dve", lambda e: e.tensor_tensor(out=self.LB[:], in0=self.vecs[:, V_LB0:V_LB0 + 8],
                                              in1=self.vecs[:, V_LB1:V_LB1 + 8], op=ALU.subtract),
             ["vecs"], ["LB"])
        S.op("act", lambda e: e.activation(out=self.LB[:], in_=self.LB[:], func=AF.Exp), ["LB"], ["LB"])
        S.op("dve", lambda e: e.tensor_scalar(out=self.LB[:], in0=self.LB[:], scalar1=1.0, scalar2=None,
                                              op0=ALU.add), ["LB"], ["LB"])
        S.op("dve", lambda e: e.reciprocal(out=self.LB[:], in_=self.LB[:]), ["LB"], ["LB"])
        S.op("dve", lambda e: e.tensor_scalar(out=self.OMLB[:], in0=self.LB[:], scalar1=-1.0, scalar2=1.0,
                                              op0=ALU.mult, op1=ALU.add), ["LB"], ["OMLB"])
        S.op("dve", lambda e: e.memset(self.RESET[:], 1.0), [], ["RESET"])
        S.op("dve", lambda e: e.memset(self.RESET[:].rearrange("p (c t) -> p c t", c=8)[:, :, 0:1], 0.0),
             ["RESET"], ["RESET"])
        skeys = [("SF", h) for h in range(8)]
        if sinit_ap is None:
            S.op("pool", lambda e: e.memset(self.SF[:], 0.0), [], skeys)
        else:
            S.dma("pool", self.SF[:], sinit_ap, [], skeys, key="sinit")

    def begin_phase2(self):
        S = self.S
        skeys = [("SF", h) for h in range(8)]
        self.state_only = False
        if self.pending_exchange is None:
            S.op("pool", lambda e: e.tensor_copy(out=self.SBF[:], in_=self.SF[:]), skeys,
                 [("SBF", h) for h in range(8)])
        S.op("pool", lambda e: e.memset(self.AT[:], 0.0), [], [("AT", h) for h in range(8)])
        if not self.fused:
            S.op("pool", lambda e: e.memset(self.HID[:], 0.0), [], [("HID", m) for m in range(32)])

    def barrier(self):
        S = self.S
        keys = [k for k in S.res.keys() if k not in ("eps_t", "dum_act", "dum_dve", "dum_pool")]
        S.op("act", lambda e: e.activation(out=self.dum_act[:], in_=self.eps_t[:], func=AF.Copy),
             ["eps_t"], keys + ["dum_act"])
        S.op("dve", lambda e: e.memset(self.dum_dve[:], 0.0), [], keys + ["dum_dve"])
        S.op("pool", lambda e: e.memset(self.dum_pool[:], 0.0), [], keys + ["dum_pool"])

    def exchange_start(self, sel_d):
        nc, S = self.nc, self.S
        src = nc.dram_tensor("sx_src", [128, 1024], F32, kind="Internal").ap()
        gat = nc.dram_tensor("sx_gat", [2 * 128, 1024], F32, kind="Internal").ap()
        self.sel = self.sb("sel_sb", [128, 1], F32)
        skeys = [("SF", h) for h in range(8)]
        sff = self.SF[:].rearrange("p h e -> p (h e)")
        S.dma("act", self.sel[:], sel_d, [], ["sel"], key="misc3")
        S.dma("pool", src, sff, skeys, ["sx_src"], key="sx0")
        groups = [[2 * i, 2 * i + 1] for i in range(NCORES // 2)]
        S.cc("pool", lambda e: e.collective_compute("AllGather", ALU.bypass, replica_groups=groups,
                                                    ins=[src.opt()], outs=[gat.opt()]),
             ["sx_src"], ["sx_gat"], key="sxcc")
        self.pending_exchange = gat

    def exchange_finish(self):
        S = self.S
        gat = self.pending_exchange
        self.pending_exchange = None
        skeys = [("SF", h) for h in range(8)]
        sff = self.SF[:].rearrange("p h e -> p (h e)")
        st = self.YT[0]
        S.dma("pool", st[:], gat[0:128, :], ["sx_gat"], [("YT", 0)], key=("sxl", 0))
        S.op("dve", lambda e: e.tensor_scalar(out=sff, in0=st[:], scalar1=self.sel[:, 0:1], scalar2=None,
                                              op0=ALU.mult), [("YT", 0), "sel"], skeys)
        S.op("pool", lambda e: e.tensor_copy(out=self.SBF[:], in_=self.SF[:]), skeys,
             [("SBF", h) for h in range(8)])

    def _hg_head(self, h, j, wq, wqkey, wf, wfkey):
        S = self.S
        so = self.state_only
        r = self.next_rr("hg", 3)
        FG, LF, BC, KE2 = self.FG[r], self.LF[r], self.BC[r], self.KE2[r]
        kFG, kLF, kBC, kKE2 = ("FG", r), ("LF", r), ("BC", r), ("KE2", r)
        if not so:
            QS, EB, KE = self.QS[r], self.EB[r], self.KE[r]
            kQS, kEB, kKE = ("QS", r), ("EB", r), ("KE", r)

            def evq(pbk, pkey):
                S.op("act", lambda e: e.activation(out=QS[:], in_=pbk[:], func=AF.Silu), [pkey], [kQS])
            self.proj_fm(wq, wqkey, j, 512, self.H, "H", 8, T, evq)

        def evf(pbk, pkey):
            S.op("act", lambda e: e.activation(out=FG[:], in_=pbk[:], func=AF.Sigmoid), [pkey], [kFG])
        self.proj_fm(wf, wfkey, j, 512, self.H, "H", 8, T, evf)
        S.op("dve", lambda e: e.tensor_scalar(out=FG[:], in0=FG[:], scalar1=self.OMLB[:, h:h + 1],
                                              scalar2=self.LB[:, h:h + 1], op0=ALU.mult, op1=ALU.add),
             [kFG, "LB", "OMLB"], [kFG])
        S.op("act", lambda e: e.activation(out=LF[:], in_=FG[:], func=AF.Ln), [kFG], [kLF])
        S.op("dve", lambda e: e.tensor_tensor_scan(out=BC[:], data0=self.RESET[:], data1=LF[:], initial=0.0,
                                                   op0=ALU.mult, op1=ALU.add), [kLF, "RESET"], [kBC])
        S.op("act", lambda e: e.activation(out=LF[:], in_=BC[:], func=AF.Exp, scale=-1.0), [kBC], [kLF])
        ebl = self.EBL[:, h * 8:(h + 1) * 8]
        S.op("act", lambda e: e.activation(out=ebl, in_=BC[:].rearrange("p (c t) -> p c t", c=8)[:, :, 63],
                                           func=AF.Exp), [kBC], [("EBL", h)])
        S.op("dve", lambda e: e.tensor_scalar(out=FG[:], in0=FG[:], scalar1=-1.0, scalar2=1.0, op0=ALU.mult,
                                              op1=ALU.add), [kFG], [kFG])
        if not so:
            S.op("dve", lambda e: e.tensor_tensor(out=KE[:], in0=FG[:], in1=LF[:], op=ALU.mult), [kFG, kLF], [kKE])
            S.op("act", lambda e: e.activation(out=EB[:], in_=BC[:], func=AF.Exp), [kBC], [kEB])
            S.op("dve", lambda e: e.tensor_tensor(out=self.QE[:, h, :], in0=QS[:], in1=EB[:], op=ALU.mult),
                 [kQS, kEB], [("QE", h)])
        S.op("dve", lambda e: e.tensor_tensor(
            out=LF[:].rearrange("p (c t) -> p c t", c=8), in0=LF[:].rearrange("p (c t) -> p c t", c=8),
            in1=ebl.unsqueeze(2).to_broadcast([128, 8, 64]), op=ALU.mult), [kLF, ("EBL", h)], [kLF])
        S.op("pool", lambda e: e.tensor_tensor(out=KE2[:], in0=FG[:], in1=LF[:], op=ALU.mult), [kFG, kLF], [kKE2])
        self.deferred_hg.append(lambda: self._hg_head2(h, r))

    def _hg_head2(self, h, r):
        S = self.S
        so = self.state_only
        KE2 = self.KE2[r]
        kKE2 = ("KE2", r)
        if not so:
            KE = self.KE[r]
            kKE = ("KE", r)
        tb = self.pb[3][:].bitcast(BF16)
        for cc in range(8):
            S.op("pe", lambda e, cc=cc: e.transpose(out=tb[0:64, cc * 128:(cc + 1) * 128],
                                                     in_=KE2[:, cc * 64:(cc + 1) * 64], identity=self.cstb[:, 0, :]),
                 [kKE2, "cstb"], [("pb", 3)])
        S.op("act", lambda e: e.activation(out=self.KET[0:64, h, :, :],
                                           in_=tb[0:64, :].rearrange("p (c d) -> p c d", c=8), func=AF.Copy),
             [("pb", 3)], [("KET", h)])
        if not so:
            sc = self.pb[2]
            for cc in range(8):
                S.op("pe", lambda e, cc=cc: e.matmul(sc[0:64, cc * 64:(cc + 1) * 64],
                                                     lhsT=KE[:, cc * 64:(cc + 1) * 64],
                                                     rhs=self.QE[:, h, cc * 64:(cc + 1) * 64], start=True, stop=True),
                     [kKE, ("QE", h)], [("pb", 2)])
            m64 = self.cstf[0:64, 3, 0:64].unsqueeze(1).to_broadcast([64, 8, 64])
            S.op("dve", lambda e: e.tensor_tensor(out=self.AT[0:64, h, :].rearrange("p (c t) -> p c t", c=8),
                                                  in0=sc[0:64, :].rearrange("p (c t) -> p c t", c=8), in1=m64,
                                                  op=ALU.mult), [("pb", 2), "cstf"], [("AT", h)])
            ob = 4 + (h % 2)
            for cc in range(8):
                S.op("pe", lambda e, cc=cc: e.matmul(self.pb[ob][:, cc * 64:(cc + 1) * 64],
                                                     lhsT=self.VTK[:, cc, h * 128:(h + 1) * 128],
                                                     rhs=self.AT[:, h, cc * 64:(cc + 1) * 64], start=True, stop=True),
                     [("VTK", cc, h // 4), ("AT", h)], [("pb", ob)])
            S.op("act", lambda e: e.activation(out=self.OS[:, h, :], in_=self.pb[ob][:], func=AF.Copy),
                 [("pb", ob)], [("OS", h)])

    def _hg_v(self, hh, wv, wvkey):
        for cc in range(8):
            self._hg_v1(hh, cc, wv, wvkey)

    def _hg_v1(self, hh, cc, wv, wvkey):
        S = self.S
        b = self.next_rr("proj", 2)
        pbk = self.pb[b]
        for kc in range(8):
            S.op("pe", lambda e, kc=kc: e.matmul(pbk[0:64, :], lhsT=self.H[:, kc, cc * 64:(cc + 1) * 64],
                                                 rhs=wv[:, kc * 512:(kc + 1) * 512], start=(kc == 0), stop=(kc == 7)),
                 [wvkey, ("H", kc)], [("pb", b)])
        S.op("act", lambda e: e.activation(out=self.VTK[0:64, cc, hh * 512:(hh + 1) * 512], in_=pbk[0:64, :],
                                           func=AF.Copy), [("pb", b)], [("VTK", cc, hh)])

    def _hg_g1(self, hh, j, wg, wgkey):
        S = self.S
        h = hh * 4 + j

        def evg(pbk, pkey):
            S.op("act", lambda e: e.activation(out=self.GS[:, h, :], in_=pbk[:], func=AF.Silu), [pkey], [("GS", h)])
        self.proj_fm(wg, wgkey, j, 512, self.H, "H", 8, T, evg)

    def _hg_chunk(self, cc):
        S = self.S
        so = self.state_only
        if not so:
            o2b = (3, 2)[cc % 2]
            for h in range(8):
                self._hg_inter(cc, h, o2b)
        for half in range(2):
            self._hg_update(cc, half)
        if not so:
            oskeys = [("OS", h) for h in range(8)]
            S.op("dve", lambda e: e.tensor_tensor(
                out=self.OS[:, :, cc * 64:(cc + 1) * 64], in0=self.pb[o2b][:, :].rearrange("p (h t) -> p h t", h=8),
                in1=self.OS[:, :, cc * 64:(cc + 1) * 64], op=ALU.add), [("pb", o2b)] + oskeys, oskeys)

    def _hg_inter(self, cc, h, o2b):
        S = self.S
        o2_ps = self.pb[o2b][:, h * 64:(h + 1) * 64]
        S.op("pe", lambda e: e.matmul(o2_ps, lhsT=self.SBF[:, h, :], rhs=self.QE[:, h, cc * 64:(cc + 1) * 64],
                                      start=True, stop=True), [("SBF", h), ("QE", h)], [("pb", o2b)])

    def _hg_update(self, cc, half):
        S = self.S
        so = self.state_only
        bank = (((4, 5), (7, 2)) if so else ((4, 5), (6, 7)))[cc % 2][half]
        hs = range(half * 4, half * 4 + 4)
        for j, h in enumerate(hs):
            self._hg_su(cc, h, bank, j)
        skeys = [("SF", h) for h in hs]
        sf = self.SF[:, half * 4:half * 4 + 4, :]
        ebl = self.EBL[:].rearrange("p (h c) -> p h c", h=8)[:, half * 4:half * 4 + 4, cc]
        S.op("dve", lambda e: e.tensor_tensor(out=sf, in0=sf, in1=ebl.unsqueeze(2).to_broadcast([128, 4, 128]),
                                              op=ALU.mult), skeys + [("EBL", h) for h in hs], skeys)
        S.op("dve", lambda e: e.tensor_tensor(out=sf, in0=self.pb[bank][:, :].rearrange("p (h e) -> p h e", h=4),
                                              in1=sf, op=ALU.add), [("pb", bank)] + skeys, skeys)
        if not so:
            S.op("pool", lambda e: e.tensor_copy(out=self.SBF[:, half * 4:half * 4 + 4, :], in_=sf), skeys,
                 [("SBF", h) for h in hs])

    def _hg_su(self, cc, h, bank, j):
        S = self.S
        vt = self.VTK[0:64, cc, h * 128:(h + 1) * 128]
        S.op("pe", lambda e: e.matmul(self.pb[bank][:, j * 128:(j + 1) * 128], lhsT=self.KET[0:64, h, cc, :], rhs=vt,
                                      start=True, stop=True), [("KET", h), ("VTK", cc, h // 4)], [("pb", bank)])

    def hgrn_tile(self, slab_base):
        S = self.S
        so = self.state_only
        hidkeys = [("HID", m) for m in range(32)]
        S.op("act", lambda e: e.activation(out=self.dum_act[:], in_=self.eps_t[:], func=AF.Copy),
             ["eps_t"], hidkeys + ["dum_act"])
        fill = []
        wv, wvkey = self.load_slab(slab_base + 4)
        self._hg_v(0, wv, wvkey)
        for hh in range(2):
            wq = wqkey = None
            if not so:
                wq, wqkey = self.load_slab(slab_base + hh)
            wf, wfkey = self.load_slab(slab_base + 2 + hh)
            if hh == 0:
                wv1, wv1key = self.load_slab(slab_base + 5)
                fill = [(lambda cc=cc: self._hg_v1(1, cc, wv1, wv1key)) for cc in range(8)]
            elif not so:
                fill = []
                for gh in range(2):
                    wg, wgkey = self.load_slab(slab_base + 6 + gh)
                    fill += [(lambda gh=gh, j=j, wg=wg, wgkey=wgkey: self._hg_g1(gh, j, wg, wgkey)) for j in range(4)]
            for j in range(4):
                self._hg_head(hh * 4 + j, j, wq, wqkey, wf, wfkey)
                for _ in range(2):
                    if fill:
                        fill.pop(0)()
                while len(self.deferred_hg) > 2:
                    self.deferred_hg.pop(0)()
        while fill:
            fill.pop(0)()
        while self.deferred_hg:
            self.deferred_hg.pop(0)()
        if self.pending_exchange is not None:
            self.exchange_finish()
        if so and self.fused:
            self.pending_chunks = [(lambda cc=cc: self._hg_chunk(cc)) for cc in range(8)]
        else:
            for cc in range(8):
                self._hg_chunk(cc)
            self._hg_fence()

    def _hg_fence(self):
        S = self.S
        akeys = [("VTK", cc, hh) for cc in range(8) for hh in range(2)] + [("KET", h) for h in range(8)]
        S.op("pool", lambda e: e.memset(self.dum_pool[:], 0.0), [], akeys + ["dum_pool"])

    def drain_chunks(self, n=None):
        if not self.pending_chunks:
            return
        k = len(self.pending_chunks) if n is None else min(n, len(self.pending_chunks))
        for _ in range(k):
            self.pending_chunks.pop(0)()
        if not self.pending_chunks:
            self._hg_fence()

    def layer1_tile(self, t, x1_ap, out_ap, next_x1_ap=None):
        S = self.S
        xkeys = [("X", c) for c in range(8)]
        if not self.x_prefetched:
            S.dma("pool", self.X[:], x1_ap, [("x1d", t)], xkeys, key="xin")
        self.x_prefetched = False
        self.rmsnorm(V_MIX1)
        self.hgrn_tile(21)

        def post(c, rt, rkey):
            S.op("pool", lambda e: e.tensor_tensor(out=self.H[:, c, :], in0=rt[:], in1=self.GS[:, c, :], op=ALU.mult),
                 [rkey, ("GS", c)], [("H", c)])
        self.rmsnorm(V_GN, src=self.OS, srckey="OS", post=post)
        for s_ in range(2):
            wt, wkey = self.load_slab(21 + 8 + s_)
            for j in range(4):
                self._wo_chunk(wt, wkey, j, s_ * 4 + j)
        self.rmsnorm(V_MLP1)
        self.mlp(21 + 10)
        self.rmsnorm(V_FIN, dst=self.OS, dstkey="OS")
        if next_x1_ap is not None:
            S.dma("pool", self.X[:], next_x1_ap, [("x1d", t + 1)], xkeys, key="xin")
            self.x_prefetched = True
        for b in range(4):
            self._store_blk(b, out_ap)

    def _wo_chunk(self, wt, wkey, j, m):
        S = self.S

        def evac(pbk, pkey):
            S.op("dve", lambda e: e.tensor_tensor(out=self.X[:, m, :], in0=pbk[:], in1=self.X[:, m, :], op=ALU.add),
                 [pkey, ("X", m)], [("X", m)])
        self.proj_fm(wt, wkey, j, 512, self.H, "H", 8, T, evac)

    def _store_blk(self, b, out_ap):
        S = self.S
        r = self.next_rr("yt", 2)
        yt = self.YT[r]
        ident = self.cstf[:, 0, :]
        for hc in range(2):
            bank = (6, 2)[hc]
            pbk = self.pb[bank]
            for c4 in range(4):
                c = hc * 4 + c4
                S.op("pe", lambda e, c=c, c4=c4, pbk=pbk: e.transpose(
                    out=pbk[:, c4 * 128:(c4 + 1) * 128], in_=self.OS[:, c, b * 128:(b + 1) * 128], identity=ident),
                     [("OS", c), "cstf"], [("pb", bank)])
            S.op("act", lambda e, hc=hc, pbk=pbk: e.activation(out=yt[:, hc * 512:(hc + 1) * 512], in_=pbk[:],
                                                               func=AF.Copy), [("pb", bank)], [("YT", r)])
        S.dma("pool", out_ap[b * 128:(b + 1) * 128, :], yt[:], [("YT", r)], [], key=("yout", r))

    def state_pass_tile(self):
        self.rmsnorm(V_MIX1)
        self.hgrn_tile(21)

    def store_x_fm(self, dst_tile_ap, key, t=0):
        S = self.S
        S.dma("pool", dst_tile_ap, self.X[:], [("X", c) for c in range(8)], [("x1d", t)], key=key)


def build_fused(ntiles=NT):
    nc = bass.Bass("TRN2", target_bir_lowering=False)
    B = Builder(nc, "fused")
    B.fused = True
    with B.es:
        B.setup_common([("w0", list(range(21))), ("w1", list(range(21, 47)))])
        B.emit_casts([0, 1, 2, 3, 4])
        B.setup_l0()
        B.setup_hgrn(False)
        B.state_only = True
        x1_d = nc.dram_tensor("x1", [NT, 128, 8, T], F32, kind="Internal").ap()
        sel_d = nc.dram_tensor("sel", [128, 1], F32, kind="ExternalInput").ap()
        out_d = nc.dram_tensor("out", [NTOK, D], F32, kind="ExternalOutput").ap()
        B.layer0_halo()
        rest = [23, 24, 25, 26, 21, 22] + list(range(27, 47))
        for t in range(ntiles):
            if t == 0:
                hq = lambda: B.emit_casts(list(range(5, 13)))
                hm = lambda: B.emit_casts(list(range(13, 21)) + rest[0:4])
                nrest = 4
            else:
                n = len(rest) if t == ntiles - 1 else 4
                hq = None
                hm = (lambda n=n: B.emit_casts(rest[:n]))
                nrest = n
            B.layer0_tile(t, hq, hm)
            rest = rest[nrest:]
            B.store_x_fm(x1_d[t], "x1out", t)
            B.state_pass_tile()
            if t + 1 < ntiles:
                B.prefetch_x(128 + (t + 1) * T)
        B.emit_casts(rest)
        B.drain_chunks()
        B.exchange_start(sel_d)
        B.barrier()
        B.begin_phase2()
        for t in range(ntiles):
            B.layer1_tile(t, x1_d[t], out_d[t * T:(t + 1) * T, :], x1_d[t + 1] if t + 1 < ntiles else None)
        B.S.emit(nc, final_waits=[("yout", 0), ("yout", 1)])
    return nc


_CACHE = {}


def kernel(**inputs):
    inp = {k: np.asarray(v) for k, v in inputs.items()}
    shared, per_core = host_prepare(inp)
    if "fused" not in _CACHE:
        _CACHE["fused"] = build_fused()
    nc = _CACHE["fused"]
    in_maps = []
    for c in range(NCORES):
        sel = np.full((128, 1), float(c % 2), np.float32)
        m = dict(w0=shared["w0"], w1=shared["w1"], vecs=shared["vecs"], rows=shared["rows"], sel=sel)
        m.update(xtok=per_core[c]["xtok"], pos=per_core[c]["pos"], cst=per_core[c]["cst"])
        in_maps.append(m)
    res = run_bass_kernel_spmd(nc, in_maps, core_ids=list(range(NCORES)))
    out = np.zeros((4, SEQ, D), np.float32)
    for c in range(NCORES):
        b, half = c // 2, c % 2
        out[b, half * NTOK:(half + 1) * NTOK] = res.results[c]["out"]
    return out
```

```python
import math
import os
from contextlib import ExitStack
import numpy as np
import concourse.bass as bass
import concourse.mybir as mybir
from concourse.bass_utils import run_bass_kernel_spmd

F32 = mybir.dt.float32
BF16 = mybir.dt.bfloat16
I32 = mybir.dt.int32
AF = mybir.ActivationFunctionType
ALU = mybir.AluOpType

NCORES = 8
D = 1024
SEQ = 8192
NTOK = 4096
T = 512
NT = NTOK // T
DFF = 4096
EPS = 1e-5
ARENA_WORDS = 17664
SAME_ENGINE_SYNC = bool(int(os.environ.get("SES", "1")))
HG_DBG = int(os.environ.get("HG_DBG", "9"))
HG_VAR = int(os.environ.get("HG_VAR", "1"))

ENGS = ("pe", "act", "dve", "pool", "sp")


class Op:
    __slots__ = ("eng", "idx", "fn", "waits", "marked", "dma_key", "dma_val", "phase", "dma_inc")


class Sched:
    def __init__(self):
        self.q = {e: [] for e in ENGS}
        self.res = {}
        self.seen = {e: {f: -1 for f in ENGS} for e in ENGS}
        self.seen_dma = {e: {} for e in ENGS}
        self.dma_count = {}
        self.phase = 0

    def _add(self, eng, fn, reads, writes, dma_key=None, dma_inc=16):
        op = Op()
        op.dma_inc = dma_inc
        op.eng, op.fn, op.waits, op.marked = eng, fn, [], False
        op.idx = len(self.q[eng])
        op.dma_key = dma_key
        op.phase = self.phase
        if dma_key is not None:
            self.dma_count[dma_key] = self.dma_count.get(dma_key, 0) + dma_inc
            op.dma_val = self.dma_count[dma_key]
        deps = {}
        for k in reads:
            r = self.res.setdefault(k, [None, []])
            if r[0] is not None:
                deps[id(r[0])] = (r[0], True)
        for k in writes:
            r = self.res.setdefault(k, [None, []])
            if r[0] is not None:
                deps[id(r[0])] = (r[0], True)
            for o in r[1]:
                if id(o) not in deps:
                    deps[id(o)] = (o, False)
        for k in reads:
            self.res[k][1].append(op)
        for k in writes:
            self.res[k][0] = op
            self.res[k][1] = []
        for d, strong in deps.values():
            if d.dma_key is not None:
                if self.seen_dma[eng].get(d.dma_key, 0) >= d.dma_val:
                    continue
                self.seen_dma[eng][d.dma_key] = d.dma_val
                op.waits.append(("dma", d.dma_key, d.dma_val))
            elif d.eng == eng:
                if eng in ("pe", "sp") or not SAME_ENGINE_SYNC:
                    continue
                if self.seen[eng][eng] >= d.idx:
                    continue
                self.seen[eng][eng] = d.idx
                d.marked = True
                op.waits.append(("eng", d))
            else:
                if self.seen[eng][d.eng] >= d.idx:
                    continue
                self.seen[eng][d.eng] = d.idx
                d.marked = True
                op.waits.append(("eng", d))
        self.q[eng].append(op)
        return op

    def op(self, eng, fn, reads=(), writes=()):
        return self._add(eng, fn, reads, writes)

    def dma(self, eng, out, in_, reads, writes, key):
        return self._add(eng, lambda e: e.dma_start(out=out, in_=in_), reads, writes, dma_key=key)

    def cc(self, eng, fn, reads, writes, key):
        return self._add(eng, fn, reads, writes, dma_key=key, dma_inc=1)

    def emit(self, nc, final_waits):
        with ExitStack() as es:
            esem = {}
            for e in ("pe", "act", "dve", "pool"):
                esem[e] = es.enter_context(nc.semaphore("sem_" + e))
            dsem = {k: es.enter_context(nc.semaphore("dsem_%s" % str(k))) for k in self.dma_count}
            val = {}
            for e in ENGS:
                c = 0
                for o in self.q[e]:
                    if o.marked:
                        c += 1
                        val[id(o)] = c
            block = es.enter_context(nc.Block())

            def run(eng_name, e):
                for o in self.q[eng_name]:
                    for w in o.waits:
                        if w[0] == "dma":
                            e.wait_ge(dsem[w[1]], w[2])
                        else:
                            e.wait_ge(esem[w[1].eng], val[id(w[1])])
                    ins = o.fn(e)
                    if o.dma_key is not None:
                        if o.dma_inc == 1:
                            ins.then_inc(dsem[o.dma_key])
                        else:
                            ins.then_inc(dsem[o.dma_key], 16)
                    elif o.marked:
                        ins.then_inc(esem[eng_name], 1)
                if eng_name == "pool":
                    for k in final_waits:
                        e.wait_ge(dsem[k], self.dma_count[k])

            @block.tensor
            def _(e):
                run("pe", e)

            @block.scalar
            def _(e):
                run("act", e)

            @block.vector
            def _(e):
                run("dve", e)

            @block.gpsimd
            def _(e):
                run("pool", e)

            @block.sync
            def _(e):
                run("sp", e)


def _slab_proj(w, c0, ncols):
    s = w[:, c0:c0 + ncols].reshape(8, 128, ncols).transpose(1, 0, 2)
    return np.ascontiguousarray(s).reshape(128, 8 * ncols)


def _slab_down(w, m):
    s = w[:, m * 128:(m + 1) * 128].reshape(32, 128, 128).transpose(1, 0, 2)
    return np.ascontiguousarray(s).reshape(128, 32 * 128)


def _qhead_of_chunk(c):
    base = 0 if c < 4 else 8
    i = c % 4
    return base + i, base + 4 + i


def _fm(v):
    return np.ascontiguousarray(v.reshape(-1, 128).T)


NV = 96
V_MIX0, V_MLP0, V_MIX1, V_MLP1, V_FIN, V_BQK, V_INVF, V_GN, V_LB0, V_LB1 = 0, 8, 16, 24, 32, 40, 50, 51, 59, 67
L0_SLABS = 21
L1_SLABS = 26


def host_prepare(inp):
    wqkv = inp["attn_w_qkv"][0]
    bqkv = inp["attn_b_qkv"][0]
    qcols = []
    for c in range(8):
        a, b = _qhead_of_chunk(c)
        qcols += list(range(a * 64, a * 64 + 64)) + list(range(b * 64, b * 64 + 64))
    qcols = np.array(qcols)
    wq_perm = wqkv[:, qcols]
    slabs0 = [_slab_proj(wq_perm, 0, 512), _slab_proj(wq_perm, 512, 512), _slab_proj(wqkv, 1024, 512)]
    wo_perm = inp["attn_w_o"][0][qcols, :]
    slabs0 += [_slab_proj(wo_perm, 0, 512), _slab_proj(wo_perm, 512, 512)]
    slabs0 += [_slab_proj(inp["mlp_w_up"][0], s * 512, 512) for s in range(8)]
    slabs0 += [_slab_down(inp["mlp_w_down"][0], m) for m in range(8)]
    win = inp["hgrn_w_in"][0]
    slabs1 = [_slab_proj(win, s * 512, 512) for s in range(8)]
    slabs1 += [_slab_proj(inp["hgrn_w_o"][0], s * 512, 512) for s in range(2)]
    slabs1 += [_slab_proj(inp["mlp_w_up"][1], s * 512, 512) for s in range(8)]
    slabs1 += [_slab_down(inp["mlp_w_down"][1], m) for m in range(8)]
    w0 = np.stack(slabs0).astype(np.float32)
    w1 = np.stack(slabs1).astype(np.float32)

    vecs = np.zeros((128, NV), np.float32)
    vecs[:, V_MIX0:V_MIX0 + 8] = _fm(inp["mix_norm"][0])
    vecs[:, V_MLP0:V_MLP0 + 8] = _fm(inp["mlp_norm"][0])
    vecs[:, V_MIX1:V_MIX1 + 8] = _fm(inp["mix_norm"][1])
    vecs[:, V_MLP1:V_MLP1 + 8] = _fm(inp["mlp_norm"][1])
    vecs[:, V_FIN:V_FIN + 8] = _fm(inp["final_norm"])
    bq_perm = bqkv[qcols]
    vecs[:, V_BQK:V_BQK + 8] = _fm(bq_perm)
    vecs[:, V_BQK + 8:V_BQK + 10] = _fm(bqkv[1024:1280])
    inv_freq = (np.float32(500000.0) ** (-np.arange(0, 16, 2, dtype=np.float32) / np.float32(16))).astype(np.float32)
    invf = np.zeros(128, np.float32)
    for p in range(128):
        if p % 64 < 16:
            invf[p] = inv_freq[p % 8]
    vecs[:, V_INVF] = invf
    vecs[:, V_GN:V_GN + 8] = _fm(inp["hgrn_g_norm"][0])
    vecs[:, V_LB0:V_LB0 + 8] = _fm(inp["hgrn_lower_bounds"][0])
    vecs[:, V_LB1:V_LB1 + 8] = _fm(inp["hgrn_lower_bounds"][1])

    rows = np.zeros((1, 272), np.float32)
    rows[0, 0:256] = bqkv[1280:1536]
    rows[0, 256:272] = inp["attn_sinks"][0]

    ident = np.eye(128, dtype=np.float32)
    R = np.zeros((128, 128), np.float32)
    for h in range(2):
        for j in range(8):
            R[h * 64 + j + 8, h * 64 + j] = -1.0
            R[h * 64 + j, h * 64 + j + 8] = 1.0
    s_idx = np.arange(128)[:, None]
    q_idx = np.arange(128)[None, :]
    maskP = (s_idx > q_idx).astype(np.float32)
    maskC = (s_idx <= q_idx).astype(np.float32)
    m128 = ((s_idx <= q_idx) & ((s_idx // 64) == (q_idx // 64))).astype(np.float32)

    shared = dict(w0=w0, w1=w1, vecs=vecs, rows=rows)
    per_core = []
    x = inp["x"]
    pos = inp["positions"]
    for c in range(NCORES):
        b, half = c // 2, c % 2
        xs = np.zeros((NTOK + 128, D), np.float32)
        ps = np.zeros((1, NTOK + 128), np.int32)
        if half == 1:
            xs[:] = x[b, NTOK - 128:SEQ]
            ps[0, :] = pos[b, NTOK - 128:SEQ]
            mp0 = maskP
        else:
            xs[128:] = x[b, 0:NTOK]
            ps[0, 128:] = pos[b, 0:NTOK]
            mp0 = np.zeros_like(maskP)
        cst = np.ascontiguousarray(np.stack([ident, R, maskP, maskC, mp0, m128], axis=1))
        per_core.append(dict(xtok=xs, pos=ps, cst=cst))
    return shared, per_core


class Builder:
    def __init__(self, nc, mode):
        self.nc = nc
        self.S = Sched()
        self.mode = mode
        self.es = ExitStack()
        self.uid = 0
        self.dbg_stage = 9
        self.arena = None
        self.aoff = {1: 0, 2: 0}
        self.deferred = []
        self.deferred_hg = []
        self.pending_exchange = None
        self.pending_chunks = []
        self.x_prefetched = False
        self.xs_pref = {}
        self.fused = False

    def sb(self, name, shape, dt):
        return self.es.enter_context(self.nc.sbuf_tensor(name, list(shape), dt))

    def aalloc(self, phase, name, shape, dt):
        if self.arena is None:
            self.arena = self.sb("arena", [128, ARENA_WORDS], F32)
        esz = 4 if dt in (F32, I32) else 2
        n = 1
        for d in shape[1:]:
            n *= d
        words = (n * esz + 3) // 4
        off = self.aoff[phase]
        self.aoff[phase] = off + words
        assert self.aoff[phase] <= ARENA_WORDS, (name, self.aoff)
        v = self.arena[:, off:off + words]
        if dt != F32:
            v = v.bitcast(dt)
        if len(shape) == 3:
            v = v.rearrange("p (a b) -> p a b", a=shape[1])
        return v

    def ps(self, name, shape, dt):
        return self.es.enter_context(self.nc.psum_tensor(name, list(shape), dt))

    def setup_common(self, wspecs):
        nc, S = self.nc, self.S
        self.wsc = {}
        self.vecs_d = nc.dram_tensor("vecs", [128, NV], F32, kind="ExternalInput").ap()
        self.cst_d = nc.dram_tensor("cst", [128, 6, 128], F32, kind="ExternalInput").ap()
        self.vecs = self.sb("vecs_sb", [128, NV], F32)
        self.cstf = self.sb("cst_f", [128, 6, 128], F32)
        self.cstb = self.sb("cst_b", [128, 6, 128], BF16)
        self.ones = self.sb("ones_b", [128, 128], BF16)
        self.ones1 = self.sb("ones1_b", [128, 128], BF16)
        self.X = self.sb("X", [128, 8, T], F32)
        self.H = self.sb("H", [128, 8, T], BF16)
        self.SQ = self.sb("SQ", [128, 8, T], BF16)
        self.RSTD = self.sb("RSTD", [128, T], F32)
        self.HID = self.sb("HID", [128, 32, T], BF16)
        self.RT = [self.sb("RT%d" % i, [128, T], F32) for i in range(2)]
        self.NSLOT = 4
        self.slab = [self.sb("slab%d" % i, [128, 4096], BF16) for i in range(self.NSLOT)]
        self.slab_i = 0
        self.pb = [self.ps("pb%d" % i, [128, 512], F32) for i in range(8)]
        self.rr = {}
        self.cast_group = {}
        order = []
        for wname, gids in wspecs:
            w_in = nc.dram_tensor(wname, [len(gids), 128, 4096], F32, kind="ExternalInput").ap()
            w_sc = nc.dram_tensor(wname + "_bf", [len(gids), 128, 4096], BF16, kind="Internal").ap()
            for i, gid in enumerate(gids):
                self.wsc[gid] = w_sc[i]
                order.append((gid, w_in[i], w_sc[i]))
        self.cast_src = {gid: (src, dst) for gid, src, dst in order}
        self.cast_batch = 0
        S.dma("act", self.vecs[:], self.vecs_d, [], ["vecs"], key="misc0")
        S.dma("act", self.cstf[:], self.cst_d, [], ["cstf"], key="misc1")
        S.op("dve", lambda e: e.tensor_copy(out=self.cstb[:], in_=self.cstf[:]), ["cstf"], ["cstb"])
        S.op("dve", lambda e: e.memset(self.ones[:], 1.0 / 1024.0), [], ["ones"])
        S.op("dve", lambda e: e.memset(self.ones1[:], 1.0), [], ["ones1"])
        self.eps_t = self.sb("eps_t", [128, 1], F32)
        S.op("dve", lambda e: e.memset(self.eps_t[:], EPS), [], ["eps_t"])

    def emit_casts(self, gids):
        S = self.S
        gids = [g for g in gids if g in self.cast_src]
        if not gids:
            return
        b = self.cast_batch
        self.cast_batch += 1
        for gid in gids:
            src, dst = self.cast_src.pop(gid)
            S.dma("pool", dst, src, reads=[], writes=[("wsc", gid)], key=("cast", b))
        last = S.res[("wsc", gids[-1])][0]
        for gid in gids:
            S.res[("wsc", gid)][0] = last

    def flush(self):
        d, self.deferred = self.deferred, []
        for fn in d:
            fn()

    def next_rr(self, name, n):
        i = self.rr.get(name, 0)
        self.rr[name] = (i + 1) % n
        return i

    def load_slab(self, s):
        S = self.S
        i = self.slab_i
        self.slab_i = (i + 1) % self.NSLOT
        key = ("slab", i)
        S.dma("sp", self.slab[i][:], self.wsc[s], reads=[("wsc", s)], writes=[key],
              key=("slabsem", i))
        return self.slab[i], key

    def rmsnorm(self, gcol, n=T, src=None, srckey="X", dst=None, dstkey="H", post=None):
        S = self.S
        X = self.X if src is None else src
        H = self.H if dst is None else dst
        SQ, RSTD = self.SQ, self.RSTD
        bank = 6
        pbk = self.pb[bank]
        for c in range(8):
            S.op("act", lambda e, c=c: e.activation(out=SQ[:, c, :n], in_=X[:, c, :n], func=AF.Square),
                 [(srckey, c)], [("SQ", c)])
        for c in range(8):
            S.op("pe", lambda e, c=c: e.matmul(pbk[:, :n], lhsT=self.ones[:], rhs=SQ[:, c, :n],
                                                 start=(c == 0), stop=(c == 7)),
                 [("SQ", c), "ones"], [("pb", bank)])
        S.op("act", lambda e: e.activation(out=RSTD[:, :n], in_=pbk[:, :n], func=AF.Ln, bias=self.eps_t[:, 0:1],
                                           scale=1.0), [("pb", bank), "eps_t"], ["RSTD"])
        S.op("act", lambda e: e.activation(out=RSTD[:, :n], in_=RSTD[:, :n], func=AF.Exp, scale=-0.5),
             ["RSTD"], ["RSTD"])
        for c in range(8):
            self._rms_chunk(c, gcol, n, X, srckey, H, dstkey, post)

    def _rms_chunk(self, c, gcol, n, X, srckey, H, dstkey, post):
        S = self.S
        RSTD = self.RSTD
        if post is None:
            S.op("dve", lambda e: e.scalar_tensor_tensor(
                out=H[:, c, :n], in0=X[:, c, :n], scalar=self.vecs[:, gcol + c:gcol + c + 1],
                in1=RSTD[:, :n], op0=ALU.mult, op1=ALU.mult),
                 [(srckey, c), "RSTD", "vecs"], [(dstkey, c)])
        else:
            r = self.next_rr("rt", 2)
            rt = self.RT[r]
            S.op("dve", lambda e: e.scalar_tensor_tensor(
                out=rt[:, :n], in0=X[:, c, :n], scalar=self.vecs[:, gcol + c:gcol + c + 1],
                in1=RSTD[:, :n], op0=ALU.mult, op1=ALU.mult),
                 [(srckey, c), "RSTD", "vecs"], [("RT", r)])
            post(c, rt, ("RT", r))

    def proj_fm(self, wt, wkey, j, ncols_slab, src, srckey, nk, n, evac):
        S = self.S
        b = self.next_rr("proj", 2)
        pbk = self.pb[b]
        for kc in range(nk):
            S.op("pe", lambda e, kc=kc: e.matmul(
                pbk[:, :n], lhsT=wt[:, kc * ncols_slab + j * 128: kc * ncols_slab + (j + 1) * 128],
                rhs=src[:, kc, :n], start=(kc == 0), stop=(kc == nk - 1)),
                 [wkey, (srckey, kc)], [("pb", b)])
        self.flush()
        evac(pbk, ("pb", b))

    def mlp(self, slab0, n=T):
        S = self.S
        X, H, HID = self.X, self.H, self.HID
        for s in range(8):
            wt, wkey = self.load_slab(slab0 + s)
            for j in range(4):
                m = s * 4 + j

                def evac(pbk, pkey, m=m):
                    r = self.next_rr("rt", 2)
                    rt = self.RT[r]
                    S.op("act", lambda e: e.activation(out=rt[:, :n], in_=pbk[:, :n], func=AF.Relu),
                         [pkey], [("RT", r)])
                    S.op("pool", lambda e: e.tensor_tensor(out=HID[:, m, :n], in0=rt[:, :n], in1=rt[:, :n],
                                                           op=ALU.mult),
                         [("RT", r)], [("HID", m)])
                self.proj_fm(wt, wkey, j, 512, H, "H", 8, n, evac)
        for m in range(8):
            wt, wkey = self.load_slab(slab0 + 8 + m)

            def evac(pbk, pkey, m=m):
                S.op("dve", lambda e: e.tensor_tensor(out=X[:, m, :n], in0=pbk[:, :n], in1=X[:, m, :n],
                                                      op=ALU.add),
                     [pkey, ("X", m)], [("X", m)])
            self.proj_fm(wt, wkey, 0, 128, HID, "HID", 32, n, evac)

    def setup_l0(self):
        nc, S = self.nc, self.S
        self.x_d = nc.dram_tensor("xtok", [NTOK + 128, D], F32, kind="ExternalInput").ap()
        self.pos_d = nc.dram_tensor("pos", [1, NTOK + 128], I32, kind="ExternalInput").ap()
        self.rows_d = nc.dram_tensor("rows", [1, 272], F32, kind="ExternalInput").ap()
        A = lambda name, shape, dt: self.aalloc(1, name, shape, dt)
        self.XS = [A("XS%d" % i, [128, D], F32) for i in range(3)]
        self.Q = A("Q", [128, 8, T], BF16)
        self.KT = A("KT", [128, 2, T + 128], BF16)
        self.VT = A("VT", [128, 5, 256], BF16)
        self.AO = A("AO", [128, 8, T], BF16)
        self.QB = [A("QB%d" % i, [128, T], BF16) for i in range(2)]
        self.T1 = [A("T1%d" % i, [128, T], F32) for i in range(2)]
        self.T2 = [A("T2%d" % i, [128, T], F32) for i in range(2)]
        self.EP = [A("EP%d" % i, [128, T], BF16) for i in range(4)]
        self.PP = [A("PP%d" % i, [128, T], BF16) for i in range(4)]
        self.DEN = [A("DEN%d" % i, [128, T], F32) for i in range(2)]
        self.POSI = A("POSI", [128, T], I32)
        self.POSF = A("POSF", [128, T], F32)
        self.ANG = A("ANG", [128, T], F32)
        self.COS = A("COS", [128, T], F32)
        self.SIN = A("SIN", [128, T], F32)
        self.rows = self.sb("rows_sb", [128, 272], F32)
        self.esink = self.sb("esink", [128, 16], F32)
        self.pi_t = self.sb("pi_t", [128, 1], F32)
        S.dma("act", self.rows[:], self.rows_d.partition_broadcast(128), [], ["rows"], key="misc2")
        S.op("act", lambda e: e.activation(out=self.esink[:], in_=self.rows[:, 256:272], func=AF.Exp),
             ["rows"], ["esink"])
        S.op("dve", lambda e: e.memset(self.pi_t[:], math.pi), [], ["pi_t"])

    def load_x_fm(self, row0, nblk):
        S = self.S
        X = self.X
        ident = self.cstf[:, 0, :]
        for b in range(nblk):
            self._load_x_blk(row0, b)

    def prefetch_x(self, row0, nblk=3):
        for b in range(nblk):
            r = self.next_rr("xs", 3)
            self.S.dma("sp", self.XS[r][:], self.x_d[row0 + b * 128: row0 + (b + 1) * 128, :], [], [("XS", r)],
                       key=("xs", r))
            self.xs_pref[(row0, b)] = r

    def _load_x_blk(self, row0, b):
        S = self.S
        X = self.X
        ident = self.cstf[:, 0, :]
        if True:
            if (row0, b) in self.xs_pref:
                r = self.xs_pref.pop((row0, b))
            else:
                r = self.next_rr("xs", 3)
                S.dma("sp", self.XS[r][:], self.x_d[row0 + b * 128: row0 + (b + 1) * 128, :], [], [("XS", r)],
                      key=("xs", r))
            xs = self.XS[r]
            for hc in range(2):
                bank = (6, 2)[hc]
                pbk = self.pb[bank]
                for c4 in range(4):
                    c = hc * 4 + c4
                    S.op("pe", lambda e, c=c, c4=c4, pbk=pbk: e.transpose(
                        out=pbk[:, c4 * 128:(c4 + 1) * 128], in_=xs[:, c * 128:(c + 1) * 128], identity=ident),
                         [("XS", r), "cstf"], [("pb", bank)])
                S.op("act", lambda e, hc=hc, pbk=pbk, b=b: e.activation(
                    out=X[:, hc * 4:(hc + 1) * 4, b * 128:(b + 1) * 128],
                    in_=pbk[:, :].rearrange("p (c t) -> p c t", c=4), func=AF.Copy),
                     [("pb", bank)], [("X", hc * 4 + i) for i in range(4)])

    def rope_tables(self, col0, n):
        S = self.S
        twopi = 2.0 * math.pi
        PI_LO = 3.1415925
        ANG, U, KI = self.ANG, self.POSF, self.POSI
        S.dma("act", self.POSI[:, :n], self.pos_d[:, col0:col0 + n].partition_broadcast(128), [], ["POSI"],
              key="posi")
        S.op("dve", lambda e: e.tensor_copy(out=self.POSF[:, :n], in_=self.POSI[:, :n]), ["POSI"], ["POSF"])
        invf = self.vecs[:, V_INVF:V_INVF + 1]
        S.op("dve", lambda e: e.tensor_scalar(out=ANG[:, :n], in0=self.POSF[:, :n], scalar1=invf, scalar2=None,
                                              op0=ALU.mult), ["POSF", "vecs"], ["ANG"])
        S.op("dve", lambda e: e.tensor_scalar(out=U[:, :n], in0=ANG[:, :n], scalar1=1.0 / twopi, scalar2=None,
                                              op0=ALU.mult), ["ANG"], ["POSF"])
        S.op("dve", lambda e: e.tensor_copy(out=KI[:, :n], in_=U[:, :n]), ["POSF"], ["POSI"])
        S.op("dve", lambda e: e.tensor_copy(out=U[:, :n], in_=KI[:, :n]), ["POSI"], ["POSF"])
        S.op("dve", lambda e: e.scalar_tensor_tensor(out=ANG[:, :n], in0=U[:, :n], scalar=-twopi, in1=ANG[:, :n],
                                                     op0=ALU.mult, op1=ALU.add), ["POSF", "ANG"], ["ANG"])

        def fold(thr, op, delta):
            S.op("dve", lambda e: e.tensor_scalar(out=U[:, :n], in0=ANG[:, :n], scalar1=thr, scalar2=None, op0=op),
                 ["ANG"], ["POSF"])
            S.op("dve", lambda e: e.scalar_tensor_tensor(out=ANG[:, :n], in0=U[:, :n], scalar=delta, in1=ANG[:, :n],
                                                         op0=ALU.mult, op1=ALU.add), ["POSF", "ANG"], ["ANG"])

        def clamp():
            S.op("dve", lambda e: e.tensor_scalar(out=ANG[:, :n], in0=ANG[:, :n], scalar1=PI_LO, scalar2=-PI_LO,
                                                  op0=ALU.min, op1=ALU.max), ["ANG"], ["ANG"])
        fold(math.pi, ALU.is_gt, -twopi)
        fold(-math.pi, ALU.is_lt, twopi)
        S.op("dve", lambda e: e.tensor_scalar(out=self.COS[:, :n], in0=ANG[:, :n], scalar1=0.5 * math.pi,
                                              scalar2=None, op0=ALU.add), ["ANG"], ["COS"])
        clamp()
        S.op("act", lambda e: e.activation(out=self.SIN[:, :n], in_=ANG[:, :n], func=AF.Sin), ["ANG"], ["SIN"])
        S.op("dve", lambda e: e.tensor_copy(out=ANG[:, :n], in_=self.COS[:, :n]), ["COS", "SIN"], ["ANG"])
        fold(math.pi, ALU.is_gt, -twopi)
        clamp()
        S.op("act", lambda e: e.activation(out=self.COS[:, :n], in_=ANG[:, :n], func=AF.Sin), ["ANG"], ["COS"])

    def qk_chunk(self, wt, wkey, j, bcol, dst, dstkey, n):
        S = self.S

        def evac(pbk, pkey):
            r = self.next_rr("qb", 2)
            qb, t1, t2 = self.QB[r], self.T1[r], self.T2[r]
            S.op("act", lambda e: e.activation(out=qb[:, :n], in_=pbk[:, :n], func=AF.Identity,
                                               bias=self.vecs[:, bcol:bcol + 1], scale=1.0),
                 [pkey, "vecs"], [("QB", r)])
            bank = (6, 3)[r]
            sw = self.pb[bank]
            S.op("dve", lambda e: e.tensor_tensor(out=t1[:, :n], in0=qb[:, :n], in1=self.COS[:, :n], op=ALU.mult),
                 [("QB", r), "COS"], [("T1", r)])

            def tail():
                S.op("pe", lambda e: e.matmul(sw[:, :n], lhsT=self.cstb[:, 1, :], rhs=qb[:, :n], start=True, stop=True),
                     [("QB", r), "cstb"], [("pb", bank)])
                S.op("dve", lambda e: e.tensor_tensor(out=t2[:, :n], in0=sw[:, :n], in1=self.SIN[:, :n], op=ALU.mult),
                     [("pb", bank), "SIN"], [("T2", r)])
                S.op("pool", lambda e: e.tensor_tensor(out=dst, in0=t1[:, :n], in1=t2[:, :n], op=ALU.add),
                     [("T1", r), ("T2", r)], [dstkey])
            self.deferred.append(tail)
        self.proj_fm(wt, wkey, j, 512, self.H, "H", 8, n, evac)

    def kv_proj(self, wt, wkey, tok0, nblk, kcol0, vblk0):
        S = self.S
        n = nblk * 128
        for j in range(2):
            self.qk_chunk(wt, wkey, j, V_BQK + 8 + j, self.KT[:, j, kcol0:kcol0 + n], ("KT", j), n)
        for b in range(nblk):
            bk = self.next_rr("proj", 2)
            pbk = self.pb[bk]
            for kc in range(8):
                S.op("pe", lambda e, kc=kc, b=b, pbk=pbk: e.matmul(
                    pbk[:, :256], lhsT=self.H[:, kc, b * 128:(b + 1) * 128],
                    rhs=wt[:, kc * 512 + 256: kc * 512 + 512], start=(kc == 0), stop=(kc == 7)),
                     [wkey, ("H", kc)], [("pb", bk)])
            self.flush()
            S.op("dve", lambda e, b=b, pbk=pbk: e.tensor_tensor(
                out=self.VT[:, vblk0 + b, :], in0=pbk[:, :256], in1=self.rows[:, 0:256], op=ALU.add),
                 [("pb", bk), "rows"], [("VT", vblk0 + b)])

    def _attn_stage1(self, b, first, g, bset):
        S = self.S
        Q, KT = self.Q, self.KT
        par = g % 2
        kc = g // 2
        pr = slice(par * 64, par * 64 + 64)
        rhs_q = Q[pr, kc * 4:kc * 4 + 4, b * 128:(b + 1) * 128]
        qkeys = [("Q", kc * 4 + i) for i in range(4)]
        es = []
        for which in range(2):
            es.append(self._attn_score(b, first, which, (2, 6)[bset] + which, pr, kc, rhs_q, qkeys))
        return es

    def _attn_score(self, b, first, which, bank, pr, kc, rhs_q, qkeys):
        S = self.S
        kb = b + which
        pbk = self.pb[bank]
        S.op("pe", lambda e: e.matmul(
            pbk[:, :].rearrange("p (i t) -> p i t", i=4), lhsT=self.KT[pr, kc, kb * 128:(kb + 1) * 128],
            rhs=rhs_q, start=True, stop=True), [("KT", kc)] + qkeys, [("pb", bank)])
        r = self.next_rr("ep", 4)
        ep, pp = self.EP[r], self.PP[r]
        S.op("act", lambda e: e.activation(out=ep[:], in_=pbk[:], func=AF.Exp, scale=0.125),
             [("pb", bank)], [("EP", r)])
        mi = (4 if first and b == 0 else 2) if which == 0 else 3
        msk = self.cstb[:, mi, :].unsqueeze(1).to_broadcast([128, 4, 128])
        S.op("pool", lambda e: e.tensor_tensor(
            out=pp[:].rearrange("p (i t) -> p i t", i=4), in0=ep[:].rearrange("p (i t) -> p i t", i=4),
            in1=msk, op=ALU.mult), [("EP", r), "cstb"], [("PP", r)])
        return (pp, ("PP", r), kb)

    def _attn_stage2(self, b, g, es, oset):
        S = self.S
        VT, AO = self.VT, self.AO
        par = g % 2
        kc = g // 2
        pr = slice(par * 64, par * 64 + 64)
        vc0 = (g - par) * 64
        bo, bd = ((4, 5), (0, 1))[oset]
        for which, (pp, pkey, kb) in enumerate(es):
            S.op("pe", lambda e, pp=pp, kb=kb, which=which: e.matmul(
                self.pb[bo][:], lhsT=VT[:, kb, vc0:vc0 + 128], rhs=pp[:], start=(which == 0), stop=(which == 1)),
                 [pkey, ("VT", kb)], [("pb", bo)])
        for which, (pp, pkey, kb) in enumerate(es):
            S.op("pe", lambda e, pp=pp, which=which: e.matmul(
                self.pb[bd][:], lhsT=self.ones1[:], rhs=pp[:], start=(which == 0), stop=(which == 1)),
                 [pkey, "ones1"], [("pb", bd)])
        r = self.next_rr("den", 2)
        den = self.DEN[r]
        esk = self.esink[pr, 4 * g:4 * g + 4].unsqueeze(2).to_broadcast([64, 4, 128])
        S.op("dve", lambda e: e.tensor_tensor(
            out=den[pr, :].rearrange("p (i t) -> p i t", i=4),
            in0=self.pb[bd][pr, :].rearrange("p (i t) -> p i t", i=4), in1=esk, op=ALU.add),
             [("pb", bd), "esink"], [("DEN", r)])
        S.op("act", lambda e: e.activation(out=den[pr, :], in_=den[pr, :], func=AF.Ln), [("DEN", r)], [("DEN", r)])
        S.op("act", lambda e: e.activation(out=den[pr, :], in_=den[pr, :], func=AF.Exp, scale=-1.0),
             [("DEN", r)], [("DEN", r)])
        S.op("dve", lambda e: e.tensor_tensor(
            out=AO[pr, kc * 4:kc * 4 + 4, b * 128:(b + 1) * 128],
            in0=self.pb[bo][pr, :].rearrange("p (i t) -> p i t", i=4),
            in1=den[pr, :].rearrange("p (i t) -> p i t", i=4), op=ALU.mult),
             [("pb", bo), ("DEN", r)], [("AO", kc * 4 + i) for i in range(4)])

    def attention_tile(self, first):
        prev = None
        i = 0
        for b in range(4):
            for g in range(4):
                es = self._attn_stage1(b, first, g, i % 2)
                if prev is not None:
                    self._attn_stage2(*prev)
                prev = (b, g, es, i % 2)
                i += 1
        self._attn_stage2(*prev)

    def layer0_tile(self, t, hook_q=None, hook_mlp=None):
        S = self.S
        row0 = 128 + t * T
        self.load_x_fm(row0, 4)
        if self.dbg_stage == 0:
            return
        self.rope_tables(row0, T)
        self.rmsnorm(V_MIX0)
        for s in range(2):
            wt, wkey = self.load_slab(s)
            for j in range(4):
                c = s * 4 + j
                self.qk_chunk(wt, wkey, j, V_BQK + c, self.Q[:, c, :], ("Q", c), T)
                if c % 2 == 1:
                    self.drain_chunks(1)
        if hook_q is not None:
            hook_q()
        wt, wkey = self.load_slab(2)
        self.kv_proj(wt, wkey, 0, 4, 128, 1)
        self.attention_tile(t == 0)
        S.op("pool", lambda e: e.tensor_copy(out=self.KT[:, :, 0:128], in_=self.KT[:, :, T:T + 128]),
             [("KT", 0), ("KT", 1)], [("KT", 0), ("KT", 1)])
        S.op("pool", lambda e: e.tensor_copy(out=self.VT[:, 0, :], in_=self.VT[:, 4, :]), [("VT", 4)], [("VT", 0)])
        for s in range(2):
            wt, wkey = self.load_slab(3 + s)
            for j in range(4):
                m = s * 4 + j

                def evac(pbk, pkey, m=m):
                    S.op("dve", lambda e: e.tensor_tensor(out=self.X[:, m, :], in0=pbk[:], in1=self.X[:, m, :],
                                                          op=ALU.add),
                         [pkey, ("X", m)], [("X", m)])
                self.proj_fm(wt, wkey, j, 512, self.AO, "AO", 8, T, evac)
                if m % 2 == 1:
                    self.drain_chunks(1)
        self.drain_chunks()
        if self.dbg_stage == 1:
            return
        self.rmsnorm(V_MLP0)
        if hook_mlp is not None:
            hook_mlp()
        self.mlp(5)

    def layer0_halo(self):
        self.load_x_fm(0, 1)
        self.rope_tables(0, 128)
        self.rmsnorm(V_MIX0, n=128)
        wt, wkey = self.load_slab(2)
        self.kv_proj(wt, wkey, 0, 1, 0, 0)

    def setup_hgrn(self, state_only, sinit_ap=None):
        nc, S = self.nc, self.S
        self.state_only = state_only
        self.SF = self.sb("SF", [128, 8, 128], F32)
        self.EBL = self.sb("EBL", [128, 64], F32)
        self.LB = self.sb("LB", [128, 8], F32)
        self.OMLB = self.sb("OMLB", [128, 8], F32)
        self.RESET = self.sb("RESET", [128, T], F32)
        self.FG = [self.sb("FG%d" % i, [128, T], F32) for i in range(3)]
        self.LF = [self.sb("LF%d" % i, [128, T], F32) for i in range(3)]
        self.BC = [self.sb("BC%d" % i, [128, T], F32) for i in range(3)]
        self.KE2 = [self.sb("KE2%d" % i, [128, T], BF16) for i in range(3)]
        self.dum_pool = self.sb("dum_pool", [128, 1], F32)
        self.dum_act = self.sb("dum_act", [128, 1], F32)
        self.dum_dve = self.sb("dum_dve", [128, 1], F32)
        self.VTK = self.HID[:, 0:8, :].rearrange("p a b -> p (a b)").rearrange("p (c n) -> p c n", c=4)
        self.KET = self.HID[:, 8:16, :].rearrange("p a b -> p (a b)").rearrange("p (h c d) -> p h c d", h=8, c=4)
        if not state_only:
            A = lambda name, shape, dt: self.aalloc(2, name, shape, dt)
            self.QS = [A("QS%d" % i, [128, T], F32) for i in range(3)]
            self.EB = [A("EB%d" % i, [128, T], F32) for i in range(3)]
            self.KE = [A("KE%d" % i, [128, T], BF16) for i in range(3)]
            self.QE = A("QE", [128, 8, T], BF16)
            self.GS = A("GS", [128, 8, T], BF16)
            self.AT = A("AT", [128, 8, T], BF16)
            self.OS = A("OS", [128, 8, T], F32)
            self.SBF = A("SBF", [128, 8, 128], BF16)
            self.YT = [A("YT%d" % i, [128, D], F32) for i in range(2)]
        S.op("dve", lambda e: e.tensor_tensor(out=self.LB[:], in0=self.vecs[:, V_LB0:V_LB0 + 8],
                                              in1=self.vecs[:, V_LB1:V_LB1 + 8], op=ALU.subtract),
             ["vecs"], ["LB"])
        S.op("act", lambda e: e.activation(out=self.LB[:], in_=self.LB[:], func=AF.Exp), ["LB"], ["LB"])
        S.op("dve", lambda e: e.tensor_scalar(out=self.LB[:], in0=self.LB[:], scalar1=1.0, scalar2=None,
                                              op0=ALU.add), ["LB"], ["LB"])
        S.op("dve", lambda e: e.reciprocal(out=self.LB[:], in_=self.LB[:]), ["LB"], ["LB"])
        S.op("dve", lambda e: e.tensor_scalar(out=self.OMLB[:], in0=self.LB[:], scalar1=-1.0, scalar2=1.0,
                                              op0=ALU.mult, op1=ALU.add), ["LB"], ["OMLB"])
        S.op("dve", lambda e: e.memset(self.RESET[:], 1.0), [], ["RESET"])
        S.op("dve", lambda e: e.memset(self.RESET[:].rearrange("p (c t) -> p c t", c=8)[:, :, 0:1], 0.0),
             ["RESET"], ["RESET"])
        skeys = [("SF", h) for h in range(8)]
        if sinit_ap is None:
            S.op("pool", lambda e: e.memset(self.SF[:], 0.0), [], skeys)
        else:
            S.dma("pool", self.SF[:], sinit_ap, [], skeys, key="sinit")

    def begin_phase2(self):
        S = self.S
        skeys = [("SF", h) for h in range(8)]
        self.state_only = False
        if self.pending_exchange is None:
            S.op("pool", lambda e: e.tensor_copy(out=self.SBF[:], in_=self.SF[:]), skeys,
                 [("SBF", h) for h in range(8)])
        S.op("pool", lambda e: e.memset(self.AT[:], 0.0), [], [("AT", h) for h in range(8)])
        if not self.fused:
            S.op("pool", lambda e: e.memset(self.HID[:], 0.0), [], [("HID", m) for m in range(32)])

    def barrier(self):
        S = self.S
        keys = [k for k in S.res.keys() if k not in ("eps_t", "dum_act", "dum_dve", "dum_pool")]
        S.op("act", lambda e: e.activation(out=self.dum_act[:], in_=self.eps_t[:], func=AF.Copy),
             ["eps_t"], keys + ["dum_act"])
        S.op("dve", lambda e: e.memset(self.dum_dve[:], 0.0), [], keys + ["dum_dve"])
        S.op("pool", lambda e: e.memset(self.dum_pool[:], 0.0), [], keys + ["dum_pool"])

    def exchange_start(self, sel_d):
        nc, S = self.nc, self.S
        src = nc.dram_tensor("sx_src", [128, 1024], F32, kind="Internal").ap()
        gat = nc.dram_tensor("sx_gat", [2 * 128, 1024], F32, kind="Internal").ap()
        self.sel = self.sb("sel_sb", [128, 1], F32)
        skeys = [("SF", h) for h in range(8)]
        sff = self.SF[:].rearrange("p h e -> p (h e)")
        S.dma("act", self.sel[:], sel_d, [], ["sel"], key="misc3")
        S.dma("pool", src, sff, skeys, ["sx_src"], key="sx0")
        groups = [[2 * i, 2 * i + 1] for i in range(NCORES // 2)]
        S.cc("pool", lambda e: e.collective_compute("AllGather", ALU.bypass, replica_groups=groups,
                                                    ins=[src.opt()], outs=[gat.opt()]),
             ["sx_src"], ["sx_gat"], key="sxcc")
        self.pending_exchange = gat

    def exchange_finish(self):
        S = self.S
        gat = self.pending_exchange
        self.pending_exchange = None
        skeys = [("SF", h) for h in range(8)]
        sff = self.SF[:].rearrange("p h e -> p (h e)")
        st = self.YT[0]
        S.dma("pool", st[:], gat[0:128, :], ["sx_gat"], [("YT", 0)], key=("sxl", 0))
        S.op("dve", lambda e: e.tensor_scalar(out=sff, in0=st[:], scalar1=self.sel[:, 0:1], scalar2=None,
                                              op0=ALU.mult), [("YT", 0), "sel"], skeys)
        S.op("pool", lambda e: e.tensor_copy(out=self.SBF[:], in_=self.SF[:]), skeys,
             [("SBF", h) for h in range(8)])

    def _hg_head(self, h, j, wq, wqkey, wf, wfkey):
        S = self.S
        so = self.state_only
        r = self.next_rr("hg", 3)
        FG, LF, BC, KE2 = self.FG[r], self.LF[r], self.BC[r], self.KE2[r]
        kFG, kLF, kBC, kKE2 = ("FG", r), ("LF", r), ("BC", r), ("KE2", r)
        if not so:
            QS, EB, KE = self.QS[r], self.EB[r], self.KE[r]
            kQS, kEB, kKE = ("QS", r), ("EB", r), ("KE", r)

            def evq(pbk, pkey):
                S.op("act", lambda e: e.activation(out=QS[:], in_=pbk[:], func=AF.Silu), [pkey], [kQS])
            self.proj_fm(wq, wqkey, j, 512, self.H, "H", 8, T, evq)

        def evf(pbk, pkey):
            S.op("act", lambda e: e.activation(out=FG[:], in_=pbk[:], func=AF.Sigmoid), [pkey], [kFG])
        self.proj_fm(wf, wfkey, j, 512, self.H, "H", 8, T, evf)
        S.op("dve", lambda e: e.tensor_scalar(out=FG[:], in0=FG[:], scalar1=self.OMLB[:, h:h + 1],
                                              scalar2=self.LB[:, h:h + 1], op0=ALU.mult, op1=ALU.add),
             [kFG, "LB", "OMLB"], [kFG])
        S.op("act", lambda e: e.activation(out=LF[:], in_=FG[:], func=AF.Ln), [kFG], [kLF])
        S.op("dve", lambda e: e.tensor_tensor_scan(out=BC[:], data0=self.RESET[:], data1=LF[:], initial=0.0,
                                                   op0=ALU.mult, op1=ALU.add), [kLF, "RESET"], [kBC])
        S.op("act", lambda e: e.activation(out=LF[:], in_=BC[:], func=AF.Exp, scale=-1.0), [kBC], [kLF])
        ebl = self.EBL[:, h * 8:(h + 1) * 8]
        S.op("act", lambda e: e.activation(out=ebl, in_=BC[:].rearrange("p (c t) -> p c t", c=8)[:, :, 63],
                                           func=AF.Exp), [kBC], [("EBL", h)])
        S.op("dve", lambda e: e.tensor_scalar(out=FG[:], in0=FG[:], scalar1=-1.0, scalar2=1.0, op0=ALU.mult,
                                              op1=ALU.add), [kFG], [kFG])
        if not so:
            S.op("dve", lambda e: e.tensor_tensor(out=KE[:], in0=FG[:], in1=LF[:], op=ALU.mult), [kFG, kLF], [kKE])
            S.op("act", lambda e: e.activation(out=EB[:], in_=BC[:], func=AF.Exp), [kBC], [kEB])
            S.op("dve", lambda e: e.tensor_tensor(out=self.QE[:, h, :], in0=QS[:], in1=EB[:], op=ALU.mult),
                 [kQS, kEB], [("QE", h)])
        S.op("dve", lambda e: e.tensor_tensor(
            out=LF[:].rearrange("p (c t) -> p c t", c=8), in0=LF[:].rearrange("p (c t) -> p c t", c=8),
            in1=ebl.unsqueeze(2).to_broadcast([128, 8, 64]), op=ALU.mult), [kLF, ("EBL", h)], [kLF])
        S.op("pool", lambda e: e.tensor_tensor(out=KE2[:], in0=FG[:], in1=LF[:], op=ALU.mult), [kFG, kLF], [kKE2])
        self.deferred_hg.append(lambda: self._hg_head2(h, r))

    def _hg_head2(self, h, r):
        S = self.S
        so = self.state_only
        KE2 = self.KE2[r]
        kKE2 = ("KE2", r)
        if not so:
            KE = self.KE[r]
            kKE = ("KE", r)
        tb = self.pb[3][:].bitcast(BF16)
        for bb in range(4):
            S.op("pe", lambda e, bb=bb: e.transpose(out=tb[:, bb * 128:(bb + 1) * 128],
                                                     in_=KE2[:, bb * 128:(bb + 1) * 128], identity=self.cstb[:, 0, :]),
                 [kKE2, "cstb"], [("pb", 3)])
        S.op("act", lambda e: e.activation(out=self.KET[:, h, :, :],
                                           in_=tb[:, 0:512].rearrange("p (c d) -> p c d", c=4), func=AF.Copy),
             [("pb", 3)], [("KET", h)])
        if not so:
            sc = self.pb[2]
            for bb in range(4):
                S.op("pe", lambda e, bb=bb: e.matmul(sc[:, bb * 128:(bb + 1) * 128],
                                                     lhsT=KE[:, bb * 128:(bb + 1) * 128],
                                                     rhs=self.QE[:, h, bb * 128:(bb + 1) * 128], start=True, stop=True),
                     [kKE, ("QE", h)], [("pb", 2)])
            m128 = self.cstf[:, 5, :].unsqueeze(1).to_broadcast([128, 4, 128])
            S.op("dve", lambda e: e.tensor_tensor(out=self.AT[:, h, :].rearrange("p (c t) -> p c t", c=4),
                                                  in0=sc[:, :].rearrange("p (c t) -> p c t", c=4), in1=m128,
                                                  op=ALU.mult), [("pb", 2), "cstf"], [("AT", h)])
            ob = 4 + (h % 2)
            for bb in range(4):
                S.op("pe", lambda e, bb=bb: e.matmul(self.pb[ob][:, bb * 128:(bb + 1) * 128],
                                                     lhsT=self.VTK[:, bb, h * 128:(h + 1) * 128],
                                                     rhs=self.AT[:, h, bb * 128:(bb + 1) * 128], start=True, stop=True),
                     [("VTK", bb, h // 4), ("AT", h)], [("pb", ob)])
            S.op("act", lambda e: e.activation(out=self.OS[:, h, :], in_=self.pb[ob][:], func=AF.Copy),
                 [("pb", ob)], [("OS", h)])

    def _hg_v(self, hh, wv, wvkey):
        for bb in range(4):
            self._hg_v1(hh, bb, wv, wvkey)

    def _hg_v1(self, hh, bb, wv, wvkey):
        S = self.S
        b = self.next_rr("proj", 2)
        pbk = self.pb[b]
        for kc in range(8):
            S.op("pe", lambda e, kc=kc: e.matmul(pbk[:, :], lhsT=self.H[:, kc, bb * 128:(bb + 1) * 128],
                                                 rhs=wv[:, kc * 512:(kc + 1) * 512], start=(kc == 0), stop=(kc == 7)),
                 [wvkey, ("H", kc)], [("pb", b)])
        S.op("act", lambda e: e.activation(out=self.VTK[:, bb, hh * 512:(hh + 1) * 512], in_=pbk[:, :],
                                           func=AF.Copy), [("pb", b)], [("VTK", bb, hh)])

    def _hg_g1(self, hh, j, wg, wgkey):
        S = self.S
        h = hh * 4 + j

        def evg(pbk, pkey):
            S.op("act", lambda e: e.activation(out=self.GS[:, h, :], in_=pbk[:], func=AF.Silu), [pkey], [("GS", h)])
        self.proj_fm(wg, wgkey, j, 512, self.H, "H", 8, T, evg)

    def _hg_chunk(self, cc):
        S = self.S
        so = self.state_only
        if not so:
            o2b = (3, 2)[cc % 2]
            for h in range(8):
                self._hg_inter(cc, h, o2b)
        for half in range(2):
            self._hg_update(cc, half)
        if not so:
            oskeys = [("OS", h) for h in range(8)]
            S.op("dve", lambda e: e.tensor_tensor(
                out=self.OS[:, :, cc * 64:(cc + 1) * 64], in0=self.pb[o2b][:, :].rearrange("p (h t) -> p h t", h=8),
                in1=self.OS[:, :, cc * 64:(cc + 1) * 64], op=ALU.add), [("pb", o2b)] + oskeys, oskeys)

    def _hg_inter(self, cc, h, o2b):
        S = self.S
        o2_ps = self.pb[o2b][:, h * 64:(h + 1) * 64]
        S.op("pe", lambda e: e.matmul(o2_ps, lhsT=self.SBF[:, h, :], rhs=self.QE[:, h, cc * 64:(cc + 1) * 64],
                                      start=True, stop=True), [("SBF", h), ("QE", h)], [("pb", o2b)])

    def _hg_update(self, cc, half):
        S = self.S
        so = self.state_only
        bank = (((4, 5), (7, 2)) if so else ((4, 5), (6, 7)))[cc % 2][half]
        hs = range(half * 4, half * 4 + 4)
        for j, h in enumerate(hs):
            self._hg_su(cc, h, bank, j)
        skeys = [("SF", h) for h in hs]
        sf = self.SF[:, half * 4:half * 4 + 4, :]
        ebl = self.EBL[:].rearrange("p (h c) -> p h c", h=8)[:, half * 4:half * 4 + 4, cc]
        S.op("dve", lambda e: e.tensor_tensor(out=sf, in0=sf, in1=ebl.unsqueeze(2).to_broadcast([128, 4, 128]),
                                              op=ALU.mult), skeys + [("EBL", h) for h in hs], skeys)
        S.op("dve", lambda e: e.tensor_tensor(out=sf, in0=self.pb[bank][:, :].rearrange("p (h e) -> p h e", h=4),
                                              in1=sf, op=ALU.add), [("pb", bank)] + skeys, skeys)
        if not so:
            S.op("pool", lambda e: e.tensor_copy(out=self.SBF[:, half * 4:half * 4 + 4, :], in_=sf), skeys,
                 [("SBF", h) for h in hs])

    def _hg_su(self, cc, h, bank, j):
        S = self.S
        bb = cc // 2
        pr = slice((cc % 2) * 64, (cc % 2) * 64 + 64)
        S.op("pe", lambda e: e.matmul(self.pb[bank][:, j * 128:(j + 1) * 128], lhsT=self.KET[pr, h, bb, :],
                                      rhs=self.VTK[pr, bb, h * 128:(h + 1) * 128], start=True, stop=True),
             [("KET", h), ("VTK", bb, h // 4)], [("pb", bank)])

    def hgrn_tile(self, slab_base):
        S = self.S
        so = self.state_only
        hidkeys = [("HID", m) for m in range(32)]
        S.op("act", lambda e: e.activation(out=self.dum_act[:], in_=self.eps_t[:], func=AF.Copy),
             ["eps_t"], hidkeys + ["dum_act"])
        fill = []
        wv, wvkey = self.load_slab(slab_base + 4)
        self._hg_v(0, wv, wvkey)
        for hh in range(2):
            wq = wqkey = None
            if not so:
                wq, wqkey = self.load_slab(slab_base + hh)
            wf, wfkey = self.load_slab(slab_base + 2 + hh)
            if hh == 0:
                wv1, wv1key = self.load_slab(slab_base + 5)
                fill = [(lambda bb=bb: self._hg_v1(1, bb, wv1, wv1key)) for bb in range(4)]
            elif not so:
                fill = []
                for gh in range(2):
                    wg, wgkey = self.load_slab(slab_base + 6 + gh)
                    fill += [(lambda gh=gh, j=j, wg=wg, wgkey=wgkey: self._hg_g1(gh, j, wg, wgkey)) for j in range(4)]
            for j in range(4):
                self._hg_head(hh * 4 + j, j, wq, wqkey, wf, wfkey)
                for _ in range(1 if hh == 0 else 2):
                    if fill:
                        fill.pop(0)()
                while len(self.deferred_hg) > 2:
                    self.deferred_hg.pop(0)()
        while fill:
            fill.pop(0)()
        while self.deferred_hg:
            self.deferred_hg.pop(0)()
        if self.pending_exchange is not None:
            self.exchange_finish()
        if so and self.fused:
            self.pending_chunks = [(lambda cc=cc: self._hg_chunk(cc)) for cc in range(8)]
        else:
            for cc in range(8):
                self._hg_chunk(cc)
            self._hg_fence()

    def _hg_fence(self):
        S = self.S
        akeys = [("VTK", bb, hh) for bb in range(4) for hh in range(2)] + [("KET", h) for h in range(8)]
        S.op("pool", lambda e: e.memset(self.dum_pool[:], 0.0), [], akeys + ["dum_pool"])

    def drain_chunks(self, n=None):
        if not self.pending_chunks:
            return
        k = len(self.pending_chunks) if n is None else min(n, len(self.pending_chunks))
        for _ in range(k):
            self.pending_chunks.pop(0)()
        if not self.pending_chunks:
            self._hg_fence()

    def layer1_tile(self, t, x1_ap, out_ap, next_x1_ap=None):
        S = self.S
        xkeys = [("X", c) for c in range(8)]
        if not self.x_prefetched:
            S.dma("pool", self.X[:], x1_ap, [("x1d", t)], xkeys, key="xin")
        self.x_prefetched = False
        self.rmsnorm(V_MIX1)
        self.hgrn_tile(21)

        def post(c, rt, rkey):
            S.op("pool", lambda e: e.tensor_tensor(out=self.H[:, c, :], in0=rt[:], in1=self.GS[:, c, :], op=ALU.mult),
                 [rkey, ("GS", c)], [("H", c)])
        self.rmsnorm(V_GN, src=self.OS, srckey="OS", post=post)
        for s_ in range(2):
            wt, wkey = self.load_slab(21 + 8 + s_)
            for j in range(4):
                self._wo_chunk(wt, wkey, j, s_ * 4 + j)
        self.rmsnorm(V_MLP1)
        self.mlp(21 + 10)
        self.rmsnorm(V_FIN, dst=self.OS, dstkey="OS")
        if next_x1_ap is not None:
            S.dma("pool", self.X[:], next_x1_ap, [("x1d", t + 1)], xkeys, key="xin")
            self.x_prefetched = True
        for b in range(4):
            self._store_blk(b, out_ap)

    def _wo_chunk(self, wt, wkey, j, m):
        S = self.S

        def evac(pbk, pkey):
            S.op("dve", lambda e: e.tensor_tensor(out=self.X[:, m, :], in0=pbk[:], in1=self.X[:, m, :], op=ALU.add),
                 [pkey, ("X", m)], [("X", m)])
        self.proj_fm(wt, wkey, j, 512, self.H, "H", 8, T, evac)

    def _store_blk(self, b, out_ap):
        S = self.S
        r = self.next_rr("yt", 2)
        yt = self.YT[r]
        ident = self.cstf[:, 0, :]
        for hc in range(2):
            bank = (6, 2)[hc]
            pbk = self.pb[bank]
            for c4 in range(4):
                c = hc * 4 + c4
                S.op("pe", lambda e, c=c, c4=c4, pbk=pbk: e.transpose(
                    out=pbk[:, c4 * 128:(c4 + 1) * 128], in_=self.OS[:, c, b * 128:(b + 1) * 128], identity=ident),
                     [("OS", c), "cstf"], [("pb", bank)])
            S.op("act", lambda e, hc=hc, pbk=pbk: e.activation(out=yt[:, hc * 512:(hc + 1) * 512], in_=pbk[:],
                                                               func=AF.Copy), [("pb", bank)], [("YT", r)])
        S.dma("pool", out_ap[b * 128:(b + 1) * 128, :], yt[:], [("YT", r)], [], key=("yout", r))

    def state_pass_tile(self):
        self.rmsnorm(V_MIX1)
        self.hgrn_tile(21)

    def store_x_fm(self, dst_tile_ap, key, t=0):
        S = self.S
        S.dma("pool", dst_tile_ap, self.X[:], [("X", c) for c in range(8)], [("x1d", t)], key=key)


def build_fused(ntiles=NT):
    nc = bass.Bass("TRN2", target_bir_lowering=False)
    B = Builder(nc, "fused")
    B.fused = True
    with B.es:
        B.setup_common([("w0", list(range(21))), ("w1", list(range(21, 47)))])
        B.emit_casts([0, 1, 2, 3, 4])
        B.setup_l0()
        B.setup_hgrn(False)
        B.state_only = True
        x1_d = nc.dram_tensor("x1", [NT, 128, 8, T], F32, kind="Internal").ap()
        sel_d = nc.dram_tensor("sel", [128, 1], F32, kind="ExternalInput").ap()
        out_d = nc.dram_tensor("out", [NTOK, D], F32, kind="ExternalOutput").ap()
        B.layer0_halo()
        rest = [23, 24, 25, 26, 21, 22] + list(range(27, 47))
        for t in range(ntiles):
            if t == 0:
                hq = lambda: B.emit_casts(list(range(5, 13)))
                hm = lambda: B.emit_casts(list(range(13, 21)) + rest[0:4])
                nrest = 4
            else:
                n = len(rest) if t == ntiles - 1 else 4
                hq = None
                hm = (lambda n=n: B.emit_casts(rest[:n]))
                nrest = n
            B.layer0_tile(t, hq, hm)
            rest = rest[nrest:]
            B.store_x_fm(x1_d[t], "x1out", t)
            B.state_pass_tile()
            if t + 1 < ntiles:
                B.prefetch_x(128 + (t + 1) * T)
        B.emit_casts(rest)
        B.drain_chunks()
        B.exchange_start(sel_d)
        B.barrier()
        B.begin_phase2()
        for t in range(ntiles):
            B.layer1_tile(t, x1_d[t], out_d[t * T:(t + 1) * T, :], x1_d[t + 1] if t + 1 < ntiles else None)
        B.S.emit(nc, final_waits=[("yout", 0), ("yout", 1)])
    return nc


_CACHE = {}


def kernel(**inputs):
    inp = {k: np.asarray(v) for k, v in inputs.items()}
    shared, per_core = host_prepare(inp)
    if "fused" not in _CACHE:
        _CACHE["fused"] = build_fused()
    nc = _CACHE["fused"]
    in_maps = []
    for c in range(NCORES):
        sel = np.full((128, 1), float(c % 2), np.float32)
        m = dict(w0=shared["w0"], w1=shared["w1"], vecs=shared["vecs"], rows=shared["rows"], sel=sel)
        m.update(xtok=per_core[c]["xtok"], pos=per_core[c]["pos"], cst=per_core[c]["cst"])
        in_maps.append(m)
    res = run_bass_kernel_spmd(nc, in_maps, core_ids=list(range(NCORES)))
    out = np.zeros((4, SEQ, D), np.float32)
    for c in range(NCORES):
        b, half = c // 2, c % 2
        out[b, half * NTOK:(half + 1) * NTOK] = res.results[c]["out"]
    return out
```

```python
import math
import os
from contextlib import ExitStack
import numpy as np
import concourse.bass as bass
import concourse.mybir as mybir
from concourse.bass_utils import run_bass_kernel_spmd

F32 = mybir.dt.float32
BF16 = mybir.dt.bfloat16
I32 = mybir.dt.int32
AF = mybir.ActivationFunctionType
ALU = mybir.AluOpType

NCORES = 8
D = 1024
SEQ = 8192
NTOK = 4096
T = 512
NT = NTOK // T
DFF = 4096
EPS = 1e-5
ARENA_WORDS = 17664
SAME_ENGINE_SYNC = bool(int(os.environ.get("SES", "1")))
HG_DBG = int(os.environ.get("HG_DBG", "9"))
HG_VAR = int(os.environ.get("HG_VAR", "1"))

ENGS = ("pe", "act", "dve", "pool", "sp")


class Op:
    __slots__ = ("eng", "idx", "fn", "waits", "marked", "dma_key", "dma_val", "phase", "dma_inc")


class Sched:
    def __init__(self):
        self.q = {e: [] for e in ENGS}
        self.res = {}
        self.seen = {e: {f: -1 for f in ENGS} for e in ENGS}
        self.seen_dma = {e: {} for e in ENGS}
        self.dma_count = {}
        self.phase = 0

    def _add(self, eng, fn, reads, writes, dma_key=None, dma_inc=16):
        op = Op()
        op.dma_inc = dma_inc
        op.eng, op.fn, op.waits, op.marked = eng, fn, [], False
        op.idx = len(self.q[eng])
        op.dma_key = dma_key
        op.phase = self.phase
        if dma_key is not None:
            self.dma_count[dma_key] = self.dma_count.get(dma_key, 0) + dma_inc
            op.dma_val = self.dma_count[dma_key]
        deps = {}
        for k in reads:
            r = self.res.setdefault(k, [None, []])
            if r[0] is not None:
                deps[id(r[0])] = (r[0], True)
        for k in writes:
            r = self.res.setdefault(k, [None, []])
            if r[0] is not None:
                deps[id(r[0])] = (r[0], True)
            for o in r[1]:
                if id(o) not in deps:
                    deps[id(o)] = (o, False)
        for k in reads:
            self.res[k][1].append(op)
        for k in writes:
            self.res[k][0] = op
            self.res[k][1] = []
        for d, strong in deps.values():
            if d.dma_key is not None:
                if self.seen_dma[eng].get(d.dma_key, 0) >= d.dma_val:
                    continue
                self.seen_dma[eng][d.dma_key] = d.dma_val
                op.waits.append(("dma", d.dma_key, d.dma_val))
            elif d.eng == eng:
                if eng in ("pe", "sp") or not SAME_ENGINE_SYNC:
                    continue
                if self.seen[eng][eng] >= d.idx:
                    continue
                self.seen[eng][eng] = d.idx
                d.marked = True
                op.waits.append(("eng", d))
            else:
                if self.seen[eng][d.eng] >= d.idx:
                    continue
                self.seen[eng][d.eng] = d.idx
                d.marked = True
                op.waits.append(("eng", d))
        self.q[eng].append(op)
        return op

    def op(self, eng, fn, reads=(), writes=()):
        return self._add(eng, fn, reads, writes)

    def dma(self, eng, out, in_, reads, writes, key):
        return self._add(eng, lambda e: e.dma_start(out=out, in_=in_), reads, writes, dma_key=key)

    def cc(self, eng, fn, reads, writes, key):
        return self._add(eng, fn, reads, writes, dma_key=key, dma_inc=1)

    def emit(self, nc, final_waits):
        with ExitStack() as es:
            esem = {}
            for e in ("pe", "act", "dve", "pool"):
                esem[e] = es.enter_context(nc.semaphore("sem_" + e))
            dsem = {k: es.enter_context(nc.semaphore("dsem_%s" % str(k))) for k in self.dma_count}
            val = {}
            for e in ENGS:
                c = 0
                for o in self.q[e]:
                    if o.marked:
                        c += 1
                        val[id(o)] = c
            block = es.enter_context(nc.Block())

            def run(eng_name, e):
                for o in self.q[eng_name]:
                    for w in o.waits:
                        if w[0] == "dma":
                            e.wait_ge(dsem[w[1]], w[2])
                        else:
                            e.wait_ge(esem[w[1].eng], val[id(w[1])])
                    ins = o.fn(e)
                    if o.dma_key is not None:
                        if o.dma_inc == 1:
                            ins.then_inc(dsem[o.dma_key])
                        else:
                            ins.then_inc(dsem[o.dma_key], 16)
                    elif o.marked:
                        ins.then_inc(esem[eng_name], 1)
                if eng_name == "pool":
                    for k in final_waits:
                        e.wait_ge(dsem[k], self.dma_count[k])

            @block.tensor
            def _(e):
                run("pe", e)

            @block.scalar
            def _(e):
                run("act", e)

            @block.vector
            def _(e):
                run("dve", e)

            @block.gpsimd
            def _(e):
                run("pool", e)

            @block.sync
            def _(e):
                run("sp", e)


def _slab_proj(w, c0, ncols):
    s = w[:, c0:c0 + ncols].reshape(8, 128, ncols).transpose(1, 0, 2)
    return np.ascontiguousarray(s).reshape(128, 8 * ncols)


def _slab_down(w, m):
    s = w[:, m * 128:(m + 1) * 128].reshape(32, 128, 128).transpose(1, 0, 2)
    return np.ascontiguousarray(s).reshape(128, 32 * 128)


def _qhead_of_chunk(c):
    base = 0 if c < 4 else 8
    i = c % 4
    return base + i, base + 4 + i


def _fm(v):
    return np.ascontiguousarray(v.reshape(-1, 128).T)


NV = 96
V_MIX0, V_MLP0, V_MIX1, V_MLP1, V_FIN, V_BQK, V_INVF, V_GN, V_LB0, V_LB1 = 0, 8, 16, 24, 32, 40, 50, 51, 59, 67
L0_SLABS = 21
L1_SLABS = 26


def host_prepare(inp):
    wqkv = inp["attn_w_qkv"][0]
    bqkv = inp["attn_b_qkv"][0]
    qcols = []
    for c in range(8):
        a, b = _qhead_of_chunk(c)
        qcols += list(range(a * 64, a * 64 + 64)) + list(range(b * 64, b * 64 + 64))
    qcols = np.array(qcols)
    wq_perm = wqkv[:, qcols]
    slabs0 = [_slab_proj(wq_perm, 0, 512), _slab_proj(wq_perm, 512, 512), _slab_proj(wqkv, 1024, 512)]
    wo_perm = inp["attn_w_o"][0][qcols, :]
    slabs0 += [_slab_proj(wo_perm, 0, 512), _slab_proj(wo_perm, 512, 512)]
    slabs0 += [_slab_proj(inp["mlp_w_up"][0], s * 512, 512) for s in range(8)]
    slabs0 += [_slab_down(inp["mlp_w_down"][0], m) for m in range(8)]
    win = inp["hgrn_w_in"][0]
    slabs1 = [_slab_proj(win, s * 512, 512) for s in range(8)]
    slabs1 += [_slab_proj(inp["hgrn_w_o"][0], s * 512, 512) for s in range(2)]
    slabs1 += [_slab_proj(inp["mlp_w_up"][1], s * 512, 512) for s in range(8)]
    slabs1 += [_slab_down(inp["mlp_w_down"][1], m) for m in range(8)]
    w0 = np.stack(slabs0).astype(np.float32)
    w1 = np.stack(slabs1).astype(np.float32)

    vecs = np.zeros((128, NV), np.float32)
    vecs[:, V_MIX0:V_MIX0 + 8] = _fm(inp["mix_norm"][0])
    vecs[:, V_MLP0:V_MLP0 + 8] = _fm(inp["mlp_norm"][0])
    vecs[:, V_MIX1:V_MIX1 + 8] = _fm(inp["mix_norm"][1])
    vecs[:, V_MLP1:V_MLP1 + 8] = _fm(inp["mlp_norm"][1])
    vecs[:, V_FIN:V_FIN + 8] = _fm(inp["final_norm"])
    bq_perm = bqkv[qcols]
    vecs[:, V_BQK:V_BQK + 8] = _fm(bq_perm)
    vecs[:, V_BQK + 8:V_BQK + 10] = _fm(bqkv[1024:1280])
    inv_freq = (np.float32(500000.0) ** (-np.arange(0, 16, 2, dtype=np.float32) / np.float32(16))).astype(np.float32)
    invf = np.zeros(128, np.float32)
    for p in range(128):
        if p % 64 < 16:
            invf[p] = inv_freq[p % 8]
    vecs[:, V_INVF] = invf
    vecs[:, V_GN:V_GN + 8] = _fm(inp["hgrn_g_norm"][0])
    vecs[:, V_LB0:V_LB0 + 8] = _fm(inp["hgrn_lower_bounds"][0])
    vecs[:, V_LB1:V_LB1 + 8] = _fm(inp["hgrn_lower_bounds"][1])

    rows = np.zeros((1, 272), np.float32)
    rows[0, 0:256] = bqkv[1280:1536]
    rows[0, 256:272] = inp["attn_sinks"][0]

    ident = np.eye(128, dtype=np.float32)
    R = np.zeros((128, 128), np.float32)
    for h in range(2):
        for j in range(8):
            R[h * 64 + j + 8, h * 64 + j] = -1.0
            R[h * 64 + j, h * 64 + j + 8] = 1.0
    s_idx = np.arange(128)[:, None]
    q_idx = np.arange(128)[None, :]
    maskP = (s_idx > q_idx).astype(np.float32)
    maskC = (s_idx <= q_idx).astype(np.float32)
    m128 = ((s_idx <= q_idx) & ((s_idx // 64) == (q_idx // 64))).astype(np.float32)

    shared = dict(w0=w0, w1=w1, vecs=vecs, rows=rows)
    per_core = []
    x = inp["x"]
    pos = inp["positions"]
    for c in range(NCORES):
        b, half = c // 2, c % 2
        xs = np.zeros((NTOK + 128, D), np.float32)
        ps = np.zeros((1, NTOK + 128), np.int32)
        if half == 1:
            xs[:] = x[b, NTOK - 128:SEQ]
            ps[0, :] = pos[b, NTOK - 128:SEQ]
            mp0 = maskP
        else:
            xs[128:] = x[b, 0:NTOK]
            ps[0, 128:] = pos[b, 0:NTOK]
            mp0 = np.zeros_like(maskP)
        cst = np.ascontiguousarray(np.stack([ident, R, maskP, maskC, mp0, m128], axis=1))
        per_core.append(dict(xtok=xs, pos=ps, cst=cst))
    return shared, per_core


class Builder:
    def __init__(self, nc, mode):
        self.nc = nc
        self.S = Sched()
        self.mode = mode
        self.es = ExitStack()
        self.uid = 0
        self.dbg_stage = 9
        self.arena = None
        self.aoff = {1: 0, 2: 0}
        self.deferred = []
        self.deferred_hg = []
        self.pending_exchange = None
        self.pending_chunks = []
        self.x_prefetched = False
        self.xs_pref = {}
        self.fused = False

    def sb(self, name, shape, dt):
        return self.es.enter_context(self.nc.sbuf_tensor(name, list(shape), dt))

    def aalloc(self, phase, name, shape, dt):
        if self.arena is None:
            self.arena = self.sb("arena", [128, ARENA_WORDS], F32)
        esz = 4 if dt in (F32, I32) else 2
        n = 1
        for d in shape[1:]:
            n *= d
        words = (n * esz + 3) // 4
        off = self.aoff[phase]
        self.aoff[phase] = off + words
        assert self.aoff[phase] <= ARENA_WORDS, (name, self.aoff)
        v = self.arena[:, off:off + words]
        if dt != F32:
            v = v.bitcast(dt)
        if len(shape) == 3:
            v = v.rearrange("p (a b) -> p a b", a=shape[1])
        return v

    def ps(self, name, shape, dt):
        return self.es.enter_context(self.nc.psum_tensor(name, list(shape), dt))

    def setup_common(self, wspecs):
        nc, S = self.nc, self.S
        self.wsc = {}
        self.vecs_d = nc.dram_tensor("vecs", [128, NV], F32, kind="ExternalInput").ap()
        self.cst_d = nc.dram_tensor("cst", [128, 6, 128], F32, kind="ExternalInput").ap()
        self.vecs = self.sb("vecs_sb", [128, NV], F32)
        self.cstf = self.sb("cst_f", [128, 6, 128], F32)
        self.cstb = self.sb("cst_b", [128, 6, 128], BF16)
        self.ones = self.sb("ones_b", [128, 128], BF16)
        self.ones1 = self.sb("ones1_b", [128, 128], BF16)
        self.X = self.sb("X", [128, 8, T], F32)
        self.H = self.sb("H", [128, 8, T], BF16)
        self.SQ = self.sb("SQ", [128, 8, T], BF16)
        self.RSTD = self.sb("RSTD", [128, T], F32)
        self.HID = self.sb("HID", [128, 32, T], BF16)
        self.RT = [self.sb("RT%d" % i, [128, T], F32) for i in range(2)]
        self.NSLOT = 4
        self.slab = [self.sb("slab%d" % i, [128, 4096], BF16) for i in range(self.NSLOT)]
        self.slab_i = 0
        self.pb = [self.ps("pb%d" % i, [128, 512], F32) for i in range(8)]
        self.rr = {}
        self.cast_group = {}
        order = []
        for wname, gids in wspecs:
            w_in = nc.dram_tensor(wname, [len(gids), 128, 4096], F32, kind="ExternalInput").ap()
            w_sc = nc.dram_tensor(wname + "_bf", [len(gids), 128, 4096], BF16, kind="Internal").ap()
            for i, gid in enumerate(gids):
                self.wsc[gid] = w_sc[i]
                order.append((gid, w_in[i], w_sc[i]))
        self.cast_src = {gid: (src, dst) for gid, src, dst in order}
        self.cast_batch = 0
        S.dma("act", self.vecs[:], self.vecs_d, [], ["vecs"], key="misc0")
        S.dma("act", self.cstf[:], self.cst_d, [], ["cstf"], key="misc1")
        S.op("dve", lambda e: e.tensor_copy(out=self.cstb[:], in_=self.cstf[:]), ["cstf"], ["cstb"])
        S.op("dve", lambda e: e.memset(self.ones[:], 1.0 / 1024.0), [], ["ones"])
        S.op("dve", lambda e: e.memset(self.ones1[:], 1.0), [], ["ones1"])
        self.eps_t = self.sb("eps_t", [128, 1], F32)
        S.op("dve", lambda e: e.memset(self.eps_t[:], EPS), [], ["eps_t"])

    def emit_casts(self, gids):
        S = self.S
        gids = [g for g in gids if g in self.cast_src]
        if not gids:
            return
        b = self.cast_batch
        self.cast_batch += 1
        for gid in gids:
            src, dst = self.cast_src.pop(gid)
            S.dma("pool", dst, src, reads=[], writes=[("wsc", gid)], key=("cast", b))
        last = S.res[("wsc", gids[-1])][0]
        for gid in gids:
            S.res[("wsc", gid)][0] = last

    def flush(self):
        d, self.deferred = self.deferred, []
        for fn in d:
            fn()

    def next_rr(self, name, n):
        i = self.rr.get(name, 0)
        self.rr[name] = (i + 1) % n
        return i

    def load_slab(self, s):
        S = self.S
        i = self.slab_i
        self.slab_i = (i + 1) % self.NSLOT
        key = ("slab", i)
        S.dma("sp", self.slab[i][:], self.wsc[s], reads=[("wsc", s)], writes=[key],
              key=("slabsem", i))
        return self.slab[i], key

    def rmsnorm(self, gcol, n=T, src=None, srckey="X", dst=None, dstkey="H", post=None, presquared=False):
        S = self.S
        X = self.X if src is None else src
        H = self.H if dst is None else dst
        SQ, RSTD = self.SQ, self.RSTD
        bank = 6
        pbk = self.pb[bank]
        for c in range(0 if presquared else 8):
            S.op("act", lambda e, c=c: e.activation(out=SQ[:, c, :n], in_=X[:, c, :n], func=AF.Square),
                 [(srckey, c)], [("SQ", c)])
        for c in range(8):
            S.op("pe", lambda e, c=c: e.matmul(pbk[:, :n], lhsT=self.ones[:], rhs=SQ[:, c, :n],
                                                 start=(c == 0), stop=(c == 7)),
                 [("SQ", c), "ones"], [("pb", bank)])
        S.op("act", lambda e: e.activation(out=RSTD[:, :n], in_=pbk[:, :n], func=AF.Ln, bias=self.eps_t[:, 0:1],
                                           scale=1.0), [("pb", bank), "eps_t"], ["RSTD"])
        S.op("act", lambda e: e.activation(out=RSTD[:, :n], in_=RSTD[:, :n], func=AF.Exp, scale=-0.5),
             ["RSTD"], ["RSTD"])
        for c in range(8):
            self._rms_chunk(c, gcol, n, X, srckey, H, dstkey, post)

    def _rms_chunk(self, c, gcol, n, X, srckey, H, dstkey, post):
        S = self.S
        RSTD = self.RSTD
        if post is None:
            S.op("dve", lambda e: e.scalar_tensor_tensor(
                out=H[:, c, :n], in0=X[:, c, :n], scalar=self.vecs[:, gcol + c:gcol + c + 1],
                in1=RSTD[:, :n], op0=ALU.mult, op1=ALU.mult),
                 [(srckey, c), "RSTD", "vecs"], [(dstkey, c)])
        else:
            r = self.next_rr("rt", 2)
            rt = self.RT[r]
            S.op("dve", lambda e: e.scalar_tensor_tensor(
                out=rt[:, :n], in0=X[:, c, :n], scalar=self.vecs[:, gcol + c:gcol + c + 1],
                in1=RSTD[:, :n], op0=ALU.mult, op1=ALU.mult),
                 [(srckey, c), "RSTD", "vecs"], [("RT", r)])
            post(c, rt, ("RT", r))

    def proj_fm(self, wt, wkey, j, ncols_slab, src, srckey, nk, n, evac):
        S = self.S
        b = self.next_rr("proj", 2)
        pbk = self.pb[b]
        for kc in range(nk):
            S.op("pe", lambda e, kc=kc: e.matmul(
                pbk[:, :n], lhsT=wt[:, kc * ncols_slab + j * 128: kc * ncols_slab + (j + 1) * 128],
                rhs=src[:, kc, :n], start=(kc == 0), stop=(kc == nk - 1)),
                 [wkey, (srckey, kc)], [("pb", b)])
        self.flush()
        evac(pbk, ("pb", b))

    def mlp(self, slab0, n=T):
        S = self.S
        X, H, HID = self.X, self.H, self.HID
        for s in range(8):
            wt, wkey = self.load_slab(slab0 + s)
            for j in range(4):
                m = s * 4 + j

                def evac(pbk, pkey, m=m):
                    r = self.next_rr("rt", 2)
                    rt = self.RT[r]
                    S.op("act", lambda e: e.activation(out=rt[:, :n], in_=pbk[:, :n], func=AF.Relu),
                         [pkey], [("RT", r)])
                    S.op("pool", lambda e: e.tensor_tensor(out=HID[:, m, :n], in0=rt[:, :n], in1=rt[:, :n],
                                                           op=ALU.mult),
                         [("RT", r)], [("HID", m)])
                self.proj_fm(wt, wkey, j, 512, H, "H", 8, n, evac)
        for m in range(8):
            wt, wkey = self.load_slab(slab0 + 8 + m)

            def evac(pbk, pkey, m=m):
                S.op("dve", lambda e: e.tensor_tensor(out=X[:, m, :n], in0=pbk[:, :n], in1=X[:, m, :n],
                                                      op=ALU.add),
                     [pkey, ("X", m)], [("X", m)])
            self.proj_fm(wt, wkey, 0, 128, HID, "HID", 32, n, evac)

    def setup_l0(self):
        nc, S = self.nc, self.S
        self.x_d = nc.dram_tensor("xtok", [NTOK + 128, D], F32, kind="ExternalInput").ap()
        self.pos_d = nc.dram_tensor("pos", [1, NTOK + 128], I32, kind="ExternalInput").ap()
        self.rows_d = nc.dram_tensor("rows", [1, 272], F32, kind="ExternalInput").ap()
        A = lambda name, shape, dt: self.aalloc(1, name, shape, dt)
        self.XS = [A("XS%d" % i, [128, D], F32) for i in range(3)]
        self.Q = A("Q", [128, 8, T], BF16)
        self.KT = A("KT", [128, 2, T + 128], BF16)
        self.VT = A("VT", [128, 5, 256], BF16)
        self.AO = A("AO", [128, 8, T], BF16)
        self.QB = [A("QB%d" % i, [128, T], BF16) for i in range(2)]
        self.T1 = [A("T1%d" % i, [128, T], F32) for i in range(2)]
        self.T2 = [A("T2%d" % i, [128, T], F32) for i in range(2)]
        self.EP = [A("EP%d" % i, [128, T], BF16) for i in range(4)]
        self.PP = [A("PP%d" % i, [128, T], BF16) for i in range(4)]
        self.DEN = [A("DEN%d" % i, [128, T], F32) for i in range(2)]
        self.POSI = A("POSI", [128, T], I32)
        self.POSF = A("POSF", [128, T], F32)
        self.ANG = A("ANG", [128, T], F32)
        self.COS = A("COS", [128, T], F32)
        self.SIN = A("SIN", [128, T], F32)
        self.rows = self.sb("rows_sb", [128, 272], F32)
        self.esink = self.sb("esink", [128, 16], F32)
        self.pi_t = self.sb("pi_t", [128, 1], F32)
        S.dma("act", self.rows[:], self.rows_d.partition_broadcast(128), [], ["rows"], key="misc2")
        S.op("act", lambda e: e.activation(out=self.esink[:], in_=self.rows[:, 256:272], func=AF.Exp),
             ["rows"], ["esink"])
        S.op("dve", lambda e: e.memset(self.pi_t[:], math.pi), [], ["pi_t"])

    def load_x_fm(self, row0, nblk):
        S = self.S
        X = self.X
        ident = self.cstf[:, 0, :]
        for b in range(nblk):
            self._load_x_blk(row0, b)

    def prefetch_x(self, row0, nblk=3):
        for b in range(nblk):
            r = self.next_rr("xs", 3)
            self.S.dma("sp", self.XS[r][:], self.x_d[row0 + b * 128: row0 + (b + 1) * 128, :], [], [("XS", r)],
                       key=("xs", r))
            self.xs_pref[(row0, b)] = r

    def _load_x_blk(self, row0, b):
        S = self.S
        X = self.X
        ident = self.cstf[:, 0, :]
        if True:
            if (row0, b) in self.xs_pref:
                r = self.xs_pref.pop((row0, b))
            else:
                r = self.next_rr("xs", 3)
                S.dma("sp", self.XS[r][:], self.x_d[row0 + b * 128: row0 + (b + 1) * 128, :], [], [("XS", r)],
                      key=("xs", r))
            xs = self.XS[r]
            for hc in range(2):
                bank = (6, 2)[hc]
                pbk = self.pb[bank]
                for c4 in range(4):
                    c = hc * 4 + c4
                    S.op("pe", lambda e, c=c, c4=c4, pbk=pbk: e.transpose(
                        out=pbk[:, c4 * 128:(c4 + 1) * 128], in_=xs[:, c * 128:(c + 1) * 128], identity=ident),
                         [("XS", r), "cstf"], [("pb", bank)])
                S.op("dve", lambda e, hc=hc, pbk=pbk, b=b: e.tensor_copy(
                    out=X[:, hc * 4:(hc + 1) * 4, b * 128:(b + 1) * 128],
                    in_=pbk[:, :].rearrange("p (c t) -> p c t", c=4)),
                     [("pb", bank)], [("X", hc * 4 + i) for i in range(4)])
                S.op("act", lambda e, hc=hc, b=b: e.activation(
                    out=self.SQ[:, hc * 4:(hc + 1) * 4, b * 128:(b + 1) * 128],
                    in_=X[:, hc * 4:(hc + 1) * 4, b * 128:(b + 1) * 128], func=AF.Square),
                     [("X", hc * 4 + i) for i in range(4)], [("SQ", hc * 4 + i) for i in range(4)])

    def rope_tables(self, col0, n):
        S = self.S
        twopi = 2.0 * math.pi
        PI_LO = 3.1415925
        ANG, U, KI = self.ANG, self.POSF, self.POSI
        S.dma("act", self.POSI[:, :n], self.pos_d[:, col0:col0 + n].partition_broadcast(128), [], ["POSI"],
              key="posi")
        S.op("dve", lambda e: e.tensor_copy(out=self.POSF[:, :n], in_=self.POSI[:, :n]), ["POSI"], ["POSF"])
        invf = self.vecs[:, V_INVF:V_INVF + 1]
        S.op("dve", lambda e: e.tensor_scalar(out=ANG[:, :n], in0=self.POSF[:, :n], scalar1=invf, scalar2=None,
                                              op0=ALU.mult), ["POSF", "vecs"], ["ANG"])
        S.op("dve", lambda e: e.tensor_scalar(out=U[:, :n], in0=ANG[:, :n], scalar1=1.0 / twopi, scalar2=None,
                                              op0=ALU.mult), ["ANG"], ["POSF"])
        S.op("dve", lambda e: e.tensor_copy(out=KI[:, :n], in_=U[:, :n]), ["POSF"], ["POSI"])
        S.op("dve", lambda e: e.tensor_copy(out=U[:, :n], in_=KI[:, :n]), ["POSI"], ["POSF"])
        S.op("dve", lambda e: e.scalar_tensor_tensor(out=ANG[:, :n], in0=U[:, :n], scalar=-twopi, in1=ANG[:, :n],
                                                     op0=ALU.mult, op1=ALU.add), ["POSF", "ANG"], ["ANG"])

        def fold(thr, op, delta):
            S.op("dve", lambda e: e.tensor_scalar(out=U[:, :n], in0=ANG[:, :n], scalar1=thr, scalar2=None, op0=op),
                 ["ANG"], ["POSF"])
            S.op("dve", lambda e: e.scalar_tensor_tensor(out=ANG[:, :n], in0=U[:, :n], scalar=delta, in1=ANG[:, :n],
                                                         op0=ALU.mult, op1=ALU.add), ["POSF", "ANG"], ["ANG"])

        def clamp():
            S.op("dve", lambda e: e.tensor_scalar(out=ANG[:, :n], in0=ANG[:, :n], scalar1=PI_LO, scalar2=-PI_LO,
                                                  op0=ALU.min, op1=ALU.max), ["ANG"], ["ANG"])
        fold(math.pi, ALU.is_gt, -twopi)
        fold(-math.pi, ALU.is_lt, twopi)
        S.op("dve", lambda e: e.tensor_scalar(out=self.COS[:, :n], in0=ANG[:, :n], scalar1=0.5 * math.pi,
                                              scalar2=None, op0=ALU.add), ["ANG"], ["COS"])
        clamp()
        S.op("act", lambda e: e.activation(out=self.SIN[:, :n], in_=ANG[:, :n], func=AF.Sin), ["ANG"], ["SIN"])
        S.op("dve", lambda e: e.tensor_copy(out=ANG[:, :n], in_=self.COS[:, :n]), ["COS", "SIN"], ["ANG"])
        fold(math.pi, ALU.is_gt, -twopi)
        clamp()
        S.op("act", lambda e: e.activation(out=self.COS[:, :n], in_=ANG[:, :n], func=AF.Sin), ["ANG"], ["COS"])

    def qk_chunk(self, wt, wkey, j, bcol, dst, dstkey, n):
        S = self.S

        def evac(pbk, pkey):
            r = self.next_rr("qb", 2)
            qb, t1, t2 = self.QB[r], self.T1[r], self.T2[r]
            S.op("act", lambda e: e.activation(out=qb[:, :n], in_=pbk[:, :n], func=AF.Identity,
                                               bias=self.vecs[:, bcol:bcol + 1], scale=1.0),
                 [pkey, "vecs"], [("QB", r)])
            bank = (6, 3)[r]
            sw = self.pb[bank]
            S.op("dve", lambda e: e.tensor_tensor(out=t1[:, :n], in0=qb[:, :n], in1=self.COS[:, :n], op=ALU.mult),
                 [("QB", r), "COS"], [("T1", r)])

            def tail():
                S.op("pe", lambda e: e.matmul(sw[:, :n], lhsT=self.cstb[:, 1, :], rhs=qb[:, :n], start=True, stop=True),
                     [("QB", r), "cstb"], [("pb", bank)])
                S.op("dve", lambda e: e.tensor_tensor(out=t2[:, :n], in0=sw[:, :n], in1=self.SIN[:, :n], op=ALU.mult),
                     [("pb", bank), "SIN"], [("T2", r)])
                S.op("pool", lambda e: e.tensor_tensor(out=dst, in0=t1[:, :n], in1=t2[:, :n], op=ALU.add),
                     [("T1", r), ("T2", r)], [dstkey])
            self.deferred.append(tail)
        self.proj_fm(wt, wkey, j, 512, self.H, "H", 8, n, evac)

    def kv_proj(self, wt, wkey, tok0, nblk, kcol0, vblk0):
        S = self.S
        n = nblk * 128
        for j in range(2):
            self.qk_chunk(wt, wkey, j, V_BQK + 8 + j, self.KT[:, j, kcol0:kcol0 + n], ("KT", j), n)
        for b in range(nblk):
            bk = self.next_rr("proj", 2)
            pbk = self.pb[bk]
            for kc in range(8):
                S.op("pe", lambda e, kc=kc, b=b, pbk=pbk: e.matmul(
                    pbk[:, :256], lhsT=self.H[:, kc, b * 128:(b + 1) * 128],
                    rhs=wt[:, kc * 512 + 256: kc * 512 + 512], start=(kc == 0), stop=(kc == 7)),
                     [wkey, ("H", kc)], [("pb", bk)])
            self.flush()
            S.op("dve", lambda e, b=b, pbk=pbk: e.tensor_tensor(
                out=self.VT[:, vblk0 + b, :], in0=pbk[:, :256], in1=self.rows[:, 0:256], op=ALU.add),
                 [("pb", bk), "rows"], [("VT", vblk0 + b)])

    def _attn_stage1(self, b, first, g, bset):
        S = self.S
        Q, KT = self.Q, self.KT
        par = g % 2
        kc = g // 2
        pr = slice(par * 64, par * 64 + 64)
        rhs_q = Q[pr, kc * 4:kc * 4 + 4, b * 128:(b + 1) * 128]
        qkeys = [("Q", kc * 4 + i) for i in range(4)]
        es = []
        for which in range(2):
            es.append(self._attn_score(b, first, which, (2, 6)[bset] + which, pr, kc, rhs_q, qkeys))
        return es

    def _attn_score(self, b, first, which, bank, pr, kc, rhs_q, qkeys):
        S = self.S
        kb = b + which
        pbk = self.pb[bank]
        S.op("pe", lambda e: e.matmul(
            pbk[:, :].rearrange("p (i t) -> p i t", i=4), lhsT=self.KT[pr, kc, kb * 128:(kb + 1) * 128],
            rhs=rhs_q, start=True, stop=True), [("KT", kc)] + qkeys, [("pb", bank)])
        r = self.next_rr("ep", 4)
        ep, pp = self.EP[r], self.PP[r]
        S.op("act", lambda e: e.activation(out=ep[:], in_=pbk[:], func=AF.Exp, scale=0.125),
             [("pb", bank)], [("EP", r)])
        mi = (4 if first and b == 0 else 2) if which == 0 else 3
        msk = self.cstb[:, mi, :].unsqueeze(1).to_broadcast([128, 4, 128])
        S.op("pool", lambda e: e.tensor_tensor(
            out=pp[:].rearrange("p (i t) -> p i t", i=4), in0=ep[:].rearrange("p (i t) -> p i t", i=4),
            in1=msk, op=ALU.mult), [("EP", r), "cstb"], [("PP", r)])
        return (pp, ("PP", r), kb)

    def _attn_stage2(self, b, g, es, oset):
        S = self.S
        VT, AO = self.VT, self.AO
        par = g % 2
        kc = g // 2
        pr = slice(par * 64, par * 64 + 64)
        vc0 = (g - par) * 64
        bo, bd = ((4, 5), (0, 1))[oset]
        for which, (pp, pkey, kb) in enumerate(es):
            S.op("pe", lambda e, pp=pp, kb=kb, which=which: e.matmul(
                self.pb[bo][:], lhsT=VT[:, kb, vc0:vc0 + 128], rhs=pp[:], start=(which == 0), stop=(which == 1)),
                 [pkey, ("VT", kb)], [("pb", bo)])
        for which, (pp, pkey, kb) in enumerate(es):
            S.op("pe", lambda e, pp=pp, which=which: e.matmul(
                self.pb[bd][:], lhsT=self.ones1[:], rhs=pp[:], start=(which == 0), stop=(which == 1)),
                 [pkey, "ones1"], [("pb", bd)])
        r = self.next_rr("den", 2)
        den = self.DEN[r]
        esk = self.esink[pr, 4 * g:4 * g + 4].unsqueeze(2).to_broadcast([64, 4, 128])
        S.op("dve", lambda e: e.tensor_tensor(
            out=den[pr, :].rearrange("p (i t) -> p i t", i=4),
            in0=self.pb[bd][pr, :].rearrange("p (i t) -> p i t", i=4), in1=esk, op=ALU.add),
             [("pb", bd), "esink"], [("DEN", r)])
        S.op("act", lambda e: e.activation(out=den[pr, :], in_=den[pr, :], func=AF.Ln), [("DEN", r)], [("DEN", r)])
        S.op("act", lambda e: e.activation(out=den[pr, :], in_=den[pr, :], func=AF.Exp, scale=-1.0),
             [("DEN", r)], [("DEN", r)])
        S.op("dve", lambda e: e.tensor_tensor(
            out=AO[pr, kc * 4:kc * 4 + 4, b * 128:(b + 1) * 128],
            in0=self.pb[bo][pr, :].rearrange("p (i t) -> p i t", i=4),
            in1=den[pr, :].rearrange("p (i t) -> p i t", i=4), op=ALU.mult),
             [("pb", bo), ("DEN", r)], [("AO", kc * 4 + i) for i in range(4)])

    def attention_tile(self, first):
        prev = None
        i = 0
        for b in range(4):
            for g in range(4):
                es = self._attn_stage1(b, first, g, i % 2)
                if prev is not None:
                    self._attn_stage2(*prev)
                prev = (b, g, es, i % 2)
                i += 1
        self._attn_stage2(*prev)

    def layer0_tile(self, t, hook_q=None, hook_mlp=None, hook_x=None):
        S = self.S
        row0 = 128 + t * T
        self.load_x_fm(row0, 4)
        if self.dbg_stage == 0:
            return
        if hook_x is not None:
            hook_x()
        self.rope_tables(row0, T)
        self.rmsnorm(V_MIX0, presquared=True)
        for s in range(2):
            wt, wkey = self.load_slab(s)
            for j in range(4):
                c = s * 4 + j
                self.qk_chunk(wt, wkey, j, V_BQK + c, self.Q[:, c, :], ("Q", c), T)
                if c % 2 == 1:
                    self.drain_chunks(1)
        if hook_q is not None:
            hook_q()
        wt, wkey = self.load_slab(2)
        self.kv_proj(wt, wkey, 0, 4, 128, 1)
        self.attention_tile(t == 0)
        S.op("pool", lambda e: e.tensor_copy(out=self.KT[:, :, 0:128], in_=self.KT[:, :, T:T + 128]),
             [("KT", 0), ("KT", 1)], [("KT", 0), ("KT", 1)])
        S.op("pool", lambda e: e.tensor_copy(out=self.VT[:, 0, :], in_=self.VT[:, 4, :]), [("VT", 4)], [("VT", 0)])
        for s in range(2):
            wt, wkey = self.load_slab(3 + s)
            for j in range(4):
                m = s * 4 + j

                def evac(pbk, pkey, m=m):
                    S.op("dve", lambda e: e.tensor_tensor(out=self.X[:, m, :], in0=pbk[:], in1=self.X[:, m, :],
                                                          op=ALU.add),
                         [pkey, ("X", m)], [("X", m)])
                self.proj_fm(wt, wkey, j, 512, self.AO, "AO", 8, T, evac)
                if m % 2 == 1:
                    self.drain_chunks(1)
        self.drain_chunks()
        if self.dbg_stage == 1:
            return
        self.rmsnorm(V_MLP0)
        if hook_mlp is not None:
            hook_mlp()
        self.mlp(5)

    def layer0_halo(self):
        self.load_x_fm(0, 1)
        self.rope_tables(0, 128)
        self.rmsnorm(V_MIX0, n=128, presquared=True)
        wt, wkey = self.load_slab(2)
        self.kv_proj(wt, wkey, 0, 1, 0, 0)

    def setup_hgrn(self, state_only, sinit_ap=None):
        nc, S = self.nc, self.S
        self.state_only = state_only
        self.SF = self.sb("SF", [128, 8, 128], F32)
        self.EBL = self.sb("EBL", [128, 64], F32)
        self.LB = self.sb("LB", [128, 8], F32)
        self.OMLB = self.sb("OMLB", [128, 8], F32)
        self.RESET = self.sb("RESET", [128, T], F32)
        self.FG = [self.sb("FG%d" % i, [128, T], F32) for i in range(3)]
        self.LF = [self.sb("LF%d" % i, [128, T], F32) for i in range(3)]
        self.BC = [self.sb("BC%d" % i, [128, T], F32) for i in range(3)]
        self.KE2 = [self.sb("KE2%d" % i, [128, T], BF16) for i in range(3)]
        self.dum_pool = self.sb("dum_pool", [128, 1], F32)
        self.dum_act = self.sb("dum_act", [128, 1], F32)
        self.dum_dve = self.sb("dum_dve", [128, 1], F32)
        self.VTK = self.HID[:, 0:8, :].rearrange("p a b -> p (a b)").rearrange("p (c n) -> p c n", c=4)
        self.KET = self.HID[:, 8:16, :].rearrange("p a b -> p (a b)").rearrange("p (h c d) -> p h c d", h=8, c=4)
        if not state_only:
            A = lambda name, shape, dt: self.aalloc(2, name, shape, dt)
            self.QS = [A("QS%d" % i, [128, T], F32) for i in range(3)]
            self.EB = [A("EB%d" % i, [128, T], F32) for i in range(3)]
            self.KE = [A("KE%d" % i, [128, T], BF16) for i in range(3)]
            self.QE = A("QE", [128, 8, T], BF16)
            self.GS = A("GS", [128, 8, T], BF16)
            self.AT = A("AT", [128, 8, T], BF16)
            self.OS = A("OS", [128, 8, T], F32)
            self.SBF = A("SBF", [128, 8, 128], BF16)
            self.YT = [A("YT%d" % i, [128, D], F32) for i in range(2)]
        S.op("dve", lambda e: e.tensor_tensor(out=self.LB[:], in0=self.vecs[:, V_LB0:V_LB0 + 8],
                                              in1=self.vecs[:, V_LB1:V_LB1 + 8], op=ALU.subtract),
             ["vecs"], ["LB"])
        S.op("act", lambda e: e.activation(out=self.LB[:], in_=self.LB[:], func=AF.Exp), ["LB"], ["LB"])
        S.op("dve", lambda e: e.tensor_scalar(out=self.LB[:], in0=self.LB[:], scalar1=1.0, scalar2=None,
                                              op0=ALU.add), ["LB"], ["LB"])
        S.op("dve", lambda e: e.reciprocal(out=self.LB[:], in_=self.LB[:]), ["LB"], ["LB"])
        S.op("dve", lambda e: e.tensor_scalar(out=self.OMLB[:], in0=self.LB[:], scalar1=-1.0, scalar2=1.0,
                                              op0=ALU.mult, op1=ALU.add), ["LB"], ["OMLB"])
        S.op("dve", lambda e: e.memset(self.RESET[:], 1.0), [], ["RESET"])
        S.op("dve", lambda e: e.memset(self.RESET[:].rearrange("p (c t) -> p c t", c=8)[:, :, 0:1], 0.0),
             ["RESET"], ["RESET"])
        skeys = [("SF", h) for h in range(8)]
        if sinit_ap is None:
            S.op("pool", lambda e: e.memset(self.SF[:], 0.0), [], skeys)
        else:
            S.dma("pool", self.SF[:], sinit_ap, [], skeys, key="sinit")

    def begin_phase2(self):
        S = self.S
        skeys = [("SF", h) for h in range(8)]
        self.state_only = False
        if self.pending_exchange is None:
            S.op("pool", lambda e: e.tensor_copy(out=self.SBF[:], in_=self.SF[:]), skeys,
                 [("SBF", h) for h in range(8)])
        S.op("pool", lambda e: e.memset(self.AT[:], 0.0), [], [("AT", h) for h in range(8)])
        if not self.fused:
            S.op("pool", lambda e: e.memset(self.HID[:], 0.0), [], [("HID", m) for m in range(32)])

    def barrier(self):
        S = self.S
        keys = [k for k in S.res.keys() if k not in ("eps_t", "dum_act", "dum_dve", "dum_pool")]
        S.op("act", lambda e: e.activation(out=self.dum_act[:], in_=self.eps_t[:], func=AF.Copy),
             ["eps_t"], keys + ["dum_act"])
        S.op("dve", lambda e: e.memset(self.dum_dve[:], 0.0), [], keys + ["dum_dve"])
        S.op("pool", lambda e: e.memset(self.dum_pool[:], 0.0), [], keys + ["dum_pool"])

    def exchange_start(self, sel_d):
        nc, S = self.nc, self.S
        src = nc.dram_tensor("sx_src", [128, 1024], F32, kind="Internal").ap()
        gat = nc.dram_tensor("sx_gat", [2 * 128, 1024], F32, kind="Internal").ap()
        self.sel = self.sb("sel_sb", [128, 1], F32)
        skeys = [("SF", h) for h in range(8)]
        sff = self.SF[:].rearrange("p h e -> p (h e)")
        S.dma("act", self.sel[:], sel_d, [], ["sel"], key="misc3")
        S.dma("pool", src, sff, skeys, ["sx_src"], key="sx0")
        groups = [[2 * i, 2 * i + 1] for i in range(NCORES // 2)]
        S.cc("pool", lambda e: e.collective_compute("AllGather", ALU.bypass, replica_groups=groups,
                                                    ins=[src.opt()], outs=[gat.opt()]),
             ["sx_src"], ["sx_gat"], key="sxcc")
        self.pending_exchange = gat

    def exchange_finish(self):
        S = self.S
        gat = self.pending_exchange
        self.pending_exchange = None
        skeys = [("SF", h) for h in range(8)]
        sff = self.SF[:].rearrange("p h e -> p (h e)")
        st = self.YT[0]
        S.dma("pool", st[:], gat[0:128, :], ["sx_gat"], [("YT", 0)], key=("sxl", 0))
        S.op("dve", lambda e: e.tensor_scalar(out=sff, in0=st[:], scalar1=self.sel[:, 0:1], scalar2=None,
                                              op0=ALU.mult), [("YT", 0), "sel"], skeys)
        S.op("pool", lambda e: e.tensor_copy(out=self.SBF[:], in_=self.SF[:]), skeys,
             [("SBF", h) for h in range(8)])

    def _hg_head(self, h, j, wq, wqkey, wf, wfkey):
        S = self.S
        so = self.state_only
        r = self.next_rr("hg", 3)
        FG, LF, BC, KE2 = self.FG[r], self.LF[r], self.BC[r], self.KE2[r]
        kFG, kLF, kBC, kKE2 = ("FG", r), ("LF", r), ("BC", r), ("KE2", r)
        if not so:
            QS, EB, KE = self.QS[r], self.EB[r], self.KE[r]
            kQS, kEB, kKE = ("QS", r), ("EB", r), ("KE", r)

            def evq(pbk, pkey):
                S.op("act", lambda e: e.activation(out=QS[:], in_=pbk[:], func=AF.Silu), [pkey], [kQS])
            self.proj_fm(wq, wqkey, j, 512, self.H, "H", 8, T, evq)

        def evf(pbk, pkey):
            S.op("act", lambda e: e.activation(out=FG[:], in_=pbk[:], func=AF.Sigmoid), [pkey], [kFG])
        self.proj_fm(wf, wfkey, j, 512, self.H, "H", 8, T, evf)
        S.op("dve", lambda e: e.tensor_scalar(out=FG[:], in0=FG[:], scalar1=self.OMLB[:, h:h + 1],
                                              scalar2=self.LB[:, h:h + 1], op0=ALU.mult, op1=ALU.add),
             [kFG, "LB", "OMLB"], [kFG])
        S.op("act", lambda e: e.activation(out=LF[:], in_=FG[:], func=AF.Ln), [kFG], [kLF])
        S.op("dve", lambda e: e.tensor_tensor_scan(out=BC[:], data0=self.RESET[:], data1=LF[:], initial=0.0,
                                                   op0=ALU.mult, op1=ALU.add), [kLF, "RESET"], [kBC])
        S.op("act", lambda e: e.activation(out=LF[:], in_=BC[:], func=AF.Exp, scale=-1.0), [kBC], [kLF])
        ebl = self.EBL[:, h * 8:(h + 1) * 8]
        S.op("act", lambda e: e.activation(out=ebl, in_=BC[:].rearrange("p (c t) -> p c t", c=8)[:, :, 63],
                                           func=AF.Exp), [kBC], [("EBL", h)])
        S.op("dve", lambda e: e.tensor_scalar(out=FG[:], in0=FG[:], scalar1=-1.0, scalar2=1.0, op0=ALU.mult,
                                              op1=ALU.add), [kFG], [kFG])
        if not so:
            S.op("dve", lambda e: e.tensor_tensor(out=KE[:], in0=FG[:], in1=LF[:], op=ALU.mult), [kFG, kLF], [kKE])
            S.op("act", lambda e: e.activation(out=EB[:], in_=BC[:], func=AF.Exp), [kBC], [kEB])
            S.op("dve", lambda e: e.tensor_tensor(out=self.QE[:, h, :], in0=QS[:], in1=EB[:], op=ALU.mult),
                 [kQS, kEB], [("QE", h)])
        S.op("dve", lambda e: e.tensor_tensor(
            out=LF[:].rearrange("p (c t) -> p c t", c=8), in0=LF[:].rearrange("p (c t) -> p c t", c=8),
            in1=ebl.unsqueeze(2).to_broadcast([128, 8, 64]), op=ALU.mult), [kLF, ("EBL", h)], [kLF])
        S.op("pool", lambda e: e.tensor_tensor(out=KE2[:], in0=FG[:], in1=LF[:], op=ALU.mult), [kFG, kLF], [kKE2])
        self.deferred_hg.append(lambda: self._hg_head2(h, r))

    def _hg_head2(self, h, r):
        S = self.S
        so = self.state_only
        KE2 = self.KE2[r]
        kKE2 = ("KE2", r)
        if not so:
            KE = self.KE[r]
            kKE = ("KE", r)
        tb = self.pb[3][:].bitcast(BF16)
        for bb in range(4):
            S.op("pe", lambda e, bb=bb: e.transpose(out=tb[:, bb * 128:(bb + 1) * 128],
                                                     in_=KE2[:, bb * 128:(bb + 1) * 128], identity=self.cstb[:, 0, :]),
                 [kKE2, "cstb"], [("pb", 3)])
        S.op("act", lambda e: e.activation(out=self.KET[:, h, :, :],
                                           in_=tb[:, 0:512].rearrange("p (c d) -> p c d", c=4), func=AF.Copy),
             [("pb", 3)], [("KET", h)])
        if not so:
            sc = self.pb[2]
            for bb in range(4):
                S.op("pe", lambda e, bb=bb: e.matmul(sc[:, bb * 128:(bb + 1) * 128],
                                                     lhsT=KE[:, bb * 128:(bb + 1) * 128],
                                                     rhs=self.QE[:, h, bb * 128:(bb + 1) * 128], start=True, stop=True),
                     [kKE, ("QE", h)], [("pb", 2)])
            m128 = self.cstf[:, 5, :].unsqueeze(1).to_broadcast([128, 4, 128])
            S.op("dve", lambda e: e.tensor_tensor(out=self.AT[:, h, :].rearrange("p (c t) -> p c t", c=4),
                                                  in0=sc[:, :].rearrange("p (c t) -> p c t", c=4), in1=m128,
                                                  op=ALU.mult), [("pb", 2), "cstf"], [("AT", h)])
            ob = 4 + (h % 2)
            for bb in range(4):
                S.op("pe", lambda e, bb=bb: e.matmul(self.pb[ob][:, bb * 128:(bb + 1) * 128],
                                                     lhsT=self.VTK[:, bb, h * 128:(h + 1) * 128],
                                                     rhs=self.AT[:, h, bb * 128:(bb + 1) * 128], start=True, stop=True),
                     [("VTK", bb, h // 4), ("AT", h)], [("pb", ob)])
            S.op("act", lambda e: e.activation(out=self.OS[:, h, :], in_=self.pb[ob][:], func=AF.Copy),
                 [("pb", ob)], [("OS", h)])

    def _hg_v(self, hh, wv, wvkey):
        for bb in range(4):
            self._hg_v1(hh, bb, wv, wvkey)

    def _hg_v1(self, hh, bb, wv, wvkey):
        S = self.S
        b = self.next_rr("proj", 2)
        pbk = self.pb[b]
        for kc in range(8):
            S.op("pe", lambda e, kc=kc: e.matmul(pbk[:, :], lhsT=self.H[:, kc, bb * 128:(bb + 1) * 128],
                                                 rhs=wv[:, kc * 512:(kc + 1) * 512], start=(kc == 0), stop=(kc == 7)),
                 [wvkey, ("H", kc)], [("pb", b)])
        S.op("act", lambda e: e.activation(out=self.VTK[:, bb, hh * 512:(hh + 1) * 512], in_=pbk[:, :],
                                           func=AF.Copy), [("pb", b)], [("VTK", bb, hh)])

    def _hg_g1(self, hh, j, wg, wgkey):
        S = self.S
        h = hh * 4 + j

        def evg(pbk, pkey):
            S.op("act", lambda e: e.activation(out=self.GS[:, h, :], in_=pbk[:], func=AF.Silu), [pkey], [("GS", h)])
        self.proj_fm(wg, wgkey, j, 512, self.H, "H", 8, T, evg)

    def _hg_chunk(self, cc):
        S = self.S
        so = self.state_only
        if not so:
            o2b = (3, 2)[cc % 2]
            for h in range(8):
                self._hg_inter(cc, h, o2b)
        for half in range(2):
            self._hg_update(cc, half)
        if not so:
            oskeys = [("OS", h) for h in range(8)]
            S.op("dve", lambda e: e.tensor_tensor(
                out=self.OS[:, :, cc * 64:(cc + 1) * 64], in0=self.pb[o2b][:, :].rearrange("p (h t) -> p h t", h=8),
                in1=self.OS[:, :, cc * 64:(cc + 1) * 64], op=ALU.add), [("pb", o2b)] + oskeys, oskeys)

    def _hg_inter(self, cc, h, o2b):
        S = self.S
        o2_ps = self.pb[o2b][:, h * 64:(h + 1) * 64]
        S.op("pe", lambda e: e.matmul(o2_ps, lhsT=self.SBF[:, h, :], rhs=self.QE[:, h, cc * 64:(cc + 1) * 64],
                                      start=True, stop=True), [("SBF", h), ("QE", h)], [("pb", o2b)])

    def _hg_update(self, cc, half):
        S = self.S
        so = self.state_only
        bank = (((4, 5), (7, 2)) if so else ((4, 5), (6, 7)))[cc % 2][half]
        hs = range(half * 4, half * 4 + 4)
        for j, h in enumerate(hs):
            self._hg_su(cc, h, bank, j)
        skeys = [("SF", h) for h in hs]
        sf = self.SF[:, half * 4:half * 4 + 4, :]
        ebl = self.EBL[:].rearrange("p (h c) -> p h c", h=8)[:, half * 4:half * 4 + 4, cc]
        S.op("dve", lambda e: e.tensor_tensor(out=sf, in0=sf, in1=ebl.unsqueeze(2).to_broadcast([128, 4, 128]),
                                              op=ALU.mult), skeys + [("EBL", h) for h in hs], skeys)
        S.op("dve", lambda e: e.tensor_tensor(out=sf, in0=self.pb[bank][:, :].rearrange("p (h e) -> p h e", h=4),
                                              in1=sf, op=ALU.add), [("pb", bank)] + skeys, skeys)
        if not so:
            S.op("pool", lambda e: e.tensor_copy(out=self.SBF[:, half * 4:half * 4 + 4, :], in_=sf), skeys,
                 [("SBF", h) for h in hs])

    def _hg_su(self, cc, h, bank, j):
        S = self.S
        bb = cc // 2
        pr = slice((cc % 2) * 64, (cc % 2) * 64 + 64)
        S.op("pe", lambda e: e.matmul(self.pb[bank][:, j * 128:(j + 1) * 128], lhsT=self.KET[pr, h, bb, :],
                                      rhs=self.VTK[pr, bb, h * 128:(h + 1) * 128], start=True, stop=True),
             [("KET", h), ("VTK", bb, h // 4)], [("pb", bank)])

    def hgrn_tile(self, slab_base):
        S = self.S
        so = self.state_only
        hidkeys = [("HID", m) for m in range(32)]
        S.op("act", lambda e: e.activation(out=self.dum_act[:], in_=self.eps_t[:], func=AF.Copy),
             ["eps_t"], hidkeys + ["dum_act"])
        fill = []
        wv, wvkey = self.load_slab(slab_base + 4)
        self._hg_v(0, wv, wvkey)
        for hh in range(2):
            wq = wqkey = None
            if not so:
                wq, wqkey = self.load_slab(slab_base + hh)
            wf, wfkey = self.load_slab(slab_base + 2 + hh)
            if hh == 0:
                wv1, wv1key = self.load_slab(slab_base + 5)
                fill = [(lambda bb=bb: self._hg_v1(1, bb, wv1, wv1key)) for bb in range(4)]
            elif not so:
                fill = []
                for gh in range(2):
                    wg, wgkey = self.load_slab(slab_base + 6 + gh)
                    fill += [(lambda gh=gh, j=j, wg=wg, wgkey=wgkey: self._hg_g1(gh, j, wg, wgkey)) for j in range(4)]
            for j in range(4):
                self._hg_head(hh * 4 + j, j, wq, wqkey, wf, wfkey)
                for _ in range(1 if hh == 0 else 2):
                    if fill:
                        fill.pop(0)()
                while len(self.deferred_hg) > 2:
                    self.deferred_hg.pop(0)()
        while fill:
            fill.pop(0)()
        while self.deferred_hg:
            self.deferred_hg.pop(0)()
        if self.pending_exchange is not None:
            self.exchange_finish()
        if so and self.fused:
            self.pending_chunks = [(lambda cc=cc: self._hg_chunk(cc)) for cc in range(8)]
        else:
            for cc in range(8):
                self._hg_chunk(cc)
            self._hg_fence()

    def _hg_fence(self):
        S = self.S
        akeys = [("VTK", bb, hh) for bb in range(4) for hh in range(2)] + [("KET", h) for h in range(8)]
        S.op("pool", lambda e: e.memset(self.dum_pool[:], 0.0), [], akeys + ["dum_pool"])

    def drain_chunks(self, n=None):
        if not self.pending_chunks:
            return
        k = len(self.pending_chunks) if n is None else min(n, len(self.pending_chunks))
        for _ in range(k):
            self.pending_chunks.pop(0)()
        if not self.pending_chunks:
            self._hg_fence()

    def layer1_tile(self, t, x1_ap, out_ap, next_x1_ap=None):
        S = self.S
        xkeys = [("X", c) for c in range(8)]
        if not self.x_prefetched:
            S.dma("pool", self.X[:], x1_ap, [("x1d", t)], xkeys, key="xin")
        self.x_prefetched = False
        self.rmsnorm(V_MIX1)
        self.hgrn_tile(21)

        def post(c, rt, rkey):
            S.op("pool", lambda e: e.tensor_tensor(out=self.H[:, c, :], in0=rt[:], in1=self.GS[:, c, :], op=ALU.mult),
                 [rkey, ("GS", c)], [("H", c)])
        self.rmsnorm(V_GN, src=self.OS, srckey="OS", post=post)
        for s_ in range(2):
            wt, wkey = self.load_slab(21 + 8 + s_)
            for j in range(4):
                self._wo_chunk(wt, wkey, j, s_ * 4 + j)
        self.rmsnorm(V_MLP1)
        self.mlp(21 + 10)
        self.rmsnorm(V_FIN, dst=self.OS, dstkey="OS")
        if next_x1_ap is not None:
            S.dma("pool", self.X[:], next_x1_ap, [("x1d", t + 1)], xkeys, key="xin")
            self.x_prefetched = True
        for b in range(4):
            self._store_blk(b, out_ap)

    def _wo_chunk(self, wt, wkey, j, m):
        S = self.S

        def evac(pbk, pkey):
            S.op("dve", lambda e: e.tensor_tensor(out=self.X[:, m, :], in0=pbk[:], in1=self.X[:, m, :], op=ALU.add),
                 [pkey, ("X", m)], [("X", m)])
        self.proj_fm(wt, wkey, j, 512, self.H, "H", 8, T, evac)

    def _store_blk(self, b, out_ap):
        S = self.S
        r = self.next_rr("yt", 2)
        yt = self.YT[r]
        ident = self.cstf[:, 0, :]
        for hc in range(2):
            bank = (6, 2)[hc]
            pbk = self.pb[bank]
            for c4 in range(4):
                c = hc * 4 + c4
                S.op("pe", lambda e, c=c, c4=c4, pbk=pbk: e.transpose(
                    out=pbk[:, c4 * 128:(c4 + 1) * 128], in_=self.OS[:, c, b * 128:(b + 1) * 128], identity=ident),
                     [("OS", c), "cstf"], [("pb", bank)])
            S.op("act", lambda e, hc=hc, pbk=pbk: e.activation(out=yt[:, hc * 512:(hc + 1) * 512], in_=pbk[:],
                                                               func=AF.Copy), [("pb", bank)], [("YT", r)])
        S.dma("pool", out_ap[b * 128:(b + 1) * 128, :], yt[:], [("YT", r)], [], key=("yout", r))

    def state_pass_tile(self):
        self.rmsnorm(V_MIX1)
        self.hgrn_tile(21)

    def store_x_fm(self, dst_tile_ap, key, t=0):
        S = self.S
        S.dma("pool", dst_tile_ap, self.X[:], [("X", c) for c in range(8)], [("x1d", t)], key=key)


def build_fused(ntiles=NT):
    nc = bass.Bass("TRN2", target_bir_lowering=False)
    B = Builder(nc, "fused")
    B.fused = True
    with B.es:
        B.setup_common([("w0", list(range(21))), ("w1", list(range(21, 47)))])
        B.emit_casts([0, 1, 2, 3, 4])
        B.setup_l0()
        B.setup_hgrn(False)
        B.state_only = True
        x1_d = nc.dram_tensor("x1", [NT, 128, 8, T], F32, kind="Internal").ap()
        sel_d = nc.dram_tensor("sel", [128, 1], F32, kind="ExternalInput").ap()
        out_d = nc.dram_tensor("out", [NTOK, D], F32, kind="ExternalOutput").ap()
        B.layer0_halo()
        rest = [23, 24, 25, 26, 21, 22] + list(range(27, 47))
        for t in range(ntiles):
            if t == 0:
                hx = lambda: B.emit_casts(list(range(5, 13)))
                hq = lambda: B.emit_casts(list(range(13, 21)))
                hm = lambda: B.emit_casts(rest[0:4])
                nrest = 4
            else:
                n = len(rest) if t == ntiles - 1 else 4
                hq = hx = None
                hm = (lambda n=n: B.emit_casts(rest[:n]))
                nrest = n
            B.layer0_tile(t, hq, hm, hx)
            rest = rest[nrest:]
            B.store_x_fm(x1_d[t], "x1out", t)
            B.state_pass_tile()
            if t + 1 < ntiles:
                B.prefetch_x(128 + (t + 1) * T)
        B.emit_casts(rest)
        B.drain_chunks()
        B.exchange_start(sel_d)
        B.barrier()
        B.begin_phase2()
        for t in range(ntiles):
            B.layer1_tile(t, x1_d[t], out_d[t * T:(t + 1) * T, :], x1_d[t + 1] if t + 1 < ntiles else None)
        B.S.emit(nc, final_waits=[("yout", 0), ("yout", 1)])
    return nc


_CACHE = {}


def kernel(**inputs):
    inp = {k: np.asarray(v) for k, v in inputs.items()}
    shared, per_core = host_prepare(inp)
    if "fused" not in _CACHE:
        _CACHE["fused"] = build_fused()
    nc = _CACHE["fused"]
    in_maps = []
    for c in range(NCORES):
        sel = np.full((128, 1), float(c % 2), np.float32)
        m = dict(w0=shared["w0"], w1=shared["w1"], vecs=shared["vecs"], rows=shared["rows"], sel=sel)
        m.update(xtok=per_core[c]["xtok"], pos=per_core[c]["pos"], cst=per_core[c]["cst"])
        in_maps.append(m)
    res = run_bass_kernel_spmd(nc, in_maps, core_ids=list(range(NCORES)))
    out = np.zeros((4, SEQ, D), np.float32)
    for c in range(NCORES):
        b, half = c // 2, c % 2
        out[b, half * NTOK:(half + 1) * NTOK] = res.results[c]["out"]
    return out
```
